# Optimizing a Trainium2 kernel written in Bass

```python
import jax
import jax.numpy as jnp
from jax import lax
import numpy as np

D_MODEL = 1024
BATCH = 8
SEQ = 4096
DEPTH = 4

N_HEADS = 16
HEAD_DIM = D_MODEL // N_HEADS
N_KV_GROUPS = 4
HEADS_PER_GROUP = N_HEADS // N_KV_GROUPS
N_BRANCH = 3
CMP_LEN = 32
CMP_STRIDE = 16
CMP_HIDDEN = 4 * HEAD_DIM
SLC_LEN = 64
SLC_TOP_N = 16
WINDOW = 512
Q_BLOCK = 64
D_FF = 2816
CONV_WIDTH = 3
RMS_EPS = 1e-6
NEG_INF = -1e30
FORCE_SCORE = 1e30

kernel_name = "yoco_shortconv_nsa_macaron_trunk"


def rms_norm(x, g):
    x32 = x.astype(jnp.float32)
    y = x32 * lax.rsqrt(jnp.mean(x32 * x32, axis=-1, keepdims=True) + RMS_EPS)
    return (y * g.astype(jnp.float32)).astype(x.dtype)


def masked_softmax(s, mask):
    s = jnp.where(mask, s, NEG_INF)
    m = jnp.max(s, axis=-1, keepdims=True)
    e = jnp.where(mask, jnp.exp(s - m), 0.0)
    return e / jnp.maximum(jnp.sum(e, axis=-1, keepdims=True), 1e-30)


def alibi_slopes():
    h = jnp.arange(1, N_HEADS + 1, dtype=jnp.float32)
    return jnp.exp2(-8.0 * h / N_HEADS).reshape(N_KV_GROUPS, HEADS_PER_GROUP)


def modulate_pre(x, g_pre, shift, scale):
    return rms_norm(x, g_pre) * (1.0 + scale[:, None, :]) + shift[:, None, :]


def gated_post_add(x, y, g_post, gate, weight):
    return x + weight * gate[:, None, :] * rms_norm(y, g_post)


def swiglu_ffn(h, w_in, w_out):
    g, u = jnp.split(h @ w_in, 2, axis=-1)
    return (jax.nn.silu(g) * u) @ w_out


def causal_shift(v, n):
    if n == 0:
        return v
    return jnp.pad(v[:, :-n], ((0, 0), (n, 0), (0, 0)))


def short_conv_mixer(h, w_in, conv_w, w_out):
    b_gate, c_gate, u = jnp.split(h @ w_in, 3, axis=-1)
    v = c_gate * u
    y = sum(conv_w[k] * causal_shift(v, CONV_WIDTH - 1 - k) for k in range(CONV_WIDTH))
    return (b_gate * y) @ w_out


def build_shared_kv(x, c, kv_norm_g, kv_ada_w, kv_ada_b, kv_w, cmp_pos, cmp_w1, cmp_w2):
    B, S, _ = x.shape
    G, dh = N_KV_GROUPS, HEAD_DIM
    shift, scale = jnp.split(jax.nn.silu(c) @ kv_ada_w + kv_ada_b, 2, axis=-1)
    h = modulate_pre(x, kv_norm_g, shift, scale)
    kv = (h @ kv_w).reshape(B, S, N_BRANCH, 2, G, dh)
    kv = jnp.transpose(kv, (2, 3, 0, 4, 1, 5))
    n_cmp = (S - CMP_LEN) // CMP_STRIDE + 1
    tok = jnp.arange(n_cmp)[:, None] * CMP_STRIDE + jnp.arange(CMP_LEN)[None, :]
    blocks = kv[0][:, :, :, tok] + cmp_pos[:, None, None, None]
    blocks = blocks.reshape(2, B, G, n_cmp, CMP_LEN * dh)
    hid = jax.nn.gelu(jnp.einsum('kbgnf,kfh->kbgnh', blocks, cmp_w1))
    kv_cmp = jnp.einsum('kbgnh,khd->kbgnd', hid, cmp_w2)
    kv_slc = kv[1].reshape(2, B, G, S // SLC_LEN, SLC_LEN, dh)
    kv_win = jnp.pad(kv[2], ((0, 0), (0, 0), (0, 0), (WINDOW, 0), (0, 0)))
    return kv_cmp, kv_slc, kv_win


def nsa_mixer(h, kv_cmp, kv_slc, kv_win, w_in, w_out):
    B, S, _ = h.shape
    G, R, dh = N_KV_GROUPS, HEADS_PER_GROUP, HEAD_DIM
    f32 = jnp.float32
    proj = h @ w_in
    q = proj[..., :N_HEADS * dh].reshape(B, S, G, R, dh) * (dh ** -0.5)
    gates = jax.nn.sigmoid(proj[..., N_HEADS * dh:].astype(f32)).reshape(B, S, N_BRANCH, G, R)
    k_cmp, v_cmp = kv_cmp[0], kv_cmp[1]
    k_slc, v_slc = kv_slc[0], kv_slc[1]
    k_win, v_win = kv_win[0], kv_win[1]
    n_cmp = k_cmp.shape[2]
    n_slc = k_slc.shape[2]
    top_n = min(SLC_TOP_N, n_slc)
    cmp_start = jnp.arange(n_cmp) * CMP_STRIDE
    cmp_end = cmp_start + CMP_LEN - 1
    slc_start = jnp.arange(n_slc) * SLC_LEN
    overlap = ((cmp_start[:, None] < slc_start[None, :] + SLC_LEN)
               & (cmp_end[:, None] >= slc_start[None, :])).astype(f32)
    slopes = alibi_slopes()
    sl5 = slopes[None, :, :, None, None]
    bi = jnp.arange(B)[:, None, None, None]
    gi = jnp.arange(G)[None, :, None, None]
    blk = jnp.arange(n_slc)

    def attend_block(qi):
        q0 = qi * Q_BLOCK
        qb = lax.dynamic_slice_in_dim(q, q0, Q_BLOCK, axis=1)
        t = q0 + jnp.arange(Q_BLOCK)
        d_cmp = t[:, None] - cmp_end[None, :]
        s = jnp.einsum('bqgrd,bgnd->bgrqn', qb, k_cmp, preferred_element_type=f32)
        s = s - sl5 * d_cmp.astype(f32)
        p_cmp = masked_softmax(s, d_cmp >= 0)
        o_cmp = jnp.einsum('bgrqn,bgnd->bqgrd', p_cmp.astype(v_cmp.dtype), v_cmp)
        imp = jnp.einsum('bgrqn,nj->bgqj', p_cmp, overlap)
        cur = t // SLC_LEN
        forced = (blk[None, :] == 0) | (blk[None, :] == cur[:, None]) | (blk[None, :] == cur[:, None] - 1)
        future = slc_start[None, :] > t[:, None]
        imp = jnp.where(forced, FORCE_SCORE, jnp.where(future, NEG_INF, imp))
        _, idx = lax.top_k(imp, top_n)
        k_sel = k_slc[bi, gi, idx]
        v_sel = v_slc[bi, gi, idx]
        d_sel = t[None, None, :, None, None] - (idx[..., None] * SLC_LEN + jnp.arange(SLC_LEN))
        s = jnp.einsum('bqgrd,bgqnld->bgrqnl', qb, k_sel, preferred_element_type=f32)
        s = s - slopes[None, :, :, None, None, None] * d_sel[:, :, None].astype(f32)
        s = s.reshape(B, G, R, Q_BLOCK, top_n * SLC_LEN)
        m_sel = (d_sel >= 0)[:, :, None].reshape(B, G, 1, Q_BLOCK, top_n * SLC_LEN)
        p_sel = masked_softmax(s, m_sel).reshape(B, G, R, Q_BLOCK, top_n, SLC_LEN)
        o_sel = jnp.einsum('bgrqnl,bgqnld->bqgrd', p_sel.astype(v_sel.dtype), v_sel)
        kw = lax.dynamic_slice_in_dim(k_win, q0, WINDOW + Q_BLOCK, axis=2)
        vw = lax.dynamic_slice_in_dim(v_win, q0, WINDOW + Q_BLOCK, axis=2)
        spos = q0 - WINDOW + jnp.arange(WINDOW + Q_BLOCK)
        d_win = t[:, None] - spos[None, :]
        m_win = (d_win >= 0) & (d_win < WINDOW) & (spos[None, :] >= 0)
        s = jnp.einsum('bqgrd,bgkd->bgrqk', qb, kw, preferred_element_type=f32)
        s = s - sl5 * d_win.astype(f32)
        p_win = masked_softmax(s, m_win)
        o_win = jnp.einsum('bgrqk,bgkd->bqgrd', p_win.astype(vw.dtype), vw)
        g = lax.dynamic_slice_in_dim(gates, q0, Q_BLOCK, axis=1)
        o = (g[:, :, 0, :, :, None] * o_cmp.astype(f32)
             + g[:, :, 1, :, :, None] * o_sel.astype(f32)
             + g[:, :, 2, :, :, None] * o_win.astype(f32))
        return o.astype(h.dtype)

    o = lax.map(attend_block, jnp.arange(S // Q_BLOCK))
    o = jnp.moveaxis(o, 0, 1).reshape(B, S, N_HEADS * dh)
    return o @ w_out


def setup_inputs(seed: int = 0) -> dict:
    key = jax.random.key(seed)
    ks = jax.random.split(key, 20)
    f32 = jnp.float32
    D = D_MODEL
    n_a = DEPTH // 2
    n_b = DEPTH - n_a
    kv_cols = N_BRANCH * 2 * N_KV_GROUPS * HEAD_DIM

    def dense(k, shape, fan_in):
        return jax.random.normal(k, shape, f32) * fan_in ** -0.5

    def small(k, shape, s):
        return jax.random.normal(k, shape, f32) * s

    return {
        "x": jax.random.normal(ks[0], (BATCH, SEQ, D), f32),
        "c": jax.random.normal(ks[1], (BATCH, D), f32),
        "ada_w": dense(ks[2], (DEPTH, D, 9 * D), D),
        "ada_b": small(ks[3], (DEPTH, 9 * D), 0.01),
        "norm_g": 1.0 + small(ks[4], (DEPTH, 3, 2, D), 0.05),
        "ffn_w_in": dense(ks[5], (DEPTH, 2, D, 2 * D_FF), D),
        "ffn_w_out": dense(ks[6], (DEPTH, 2, D_FF, D), D_FF),
        "a_w_in": dense(ks[7], (n_a, D, 3 * D), D),
        "a_conv": dense(ks[8], (n_a, CONV_WIDTH, D), CONV_WIDTH),
        "a_w_out": dense(ks[9], (n_a, D, D), D),
        "kv_norm_g": 1.0 + small(ks[10], (D,), 0.05),
        "kv_ada_w": dense(ks[11], (D, 2 * D), D),
        "kv_ada_b": small(ks[12], (2 * D,), 0.01),
        "kv_w": dense(ks[13], (D, kv_cols), D),
        "cmp_pos": small(ks[14], (2, CMP_LEN, HEAD_DIM), 0.1),
        "cmp_w1": dense(ks[15], (2, CMP_LEN * HEAD_DIM, CMP_HIDDEN), CMP_LEN * HEAD_DIM),
        "cmp_w2": dense(ks[16], (2, CMP_HIDDEN, HEAD_DIM), CMP_HIDDEN),
        "b_w_in": dense(ks[17], (n_b, D, N_HEADS * HEAD_DIM + N_BRANCH * N_HEADS), D),
        "b_w_out": dense(ks[18], (n_b, N_HEADS * HEAD_DIM, D), N_HEADS * HEAD_DIM),
    }


def reference(x, c, ada_w, ada_b, norm_g, ffn_w_in, ffn_w_out, a_w_in, a_conv, a_w_out,
              kv_norm_g, kv_ada_w, kv_ada_b, kv_w, cmp_pos, cmp_w1, cmp_w2, b_w_in, b_w_out):
    B = x.shape[0]
    D = x.shape[-1]
    n_a = DEPTH // 2
    kv_cmp = kv_slc = kv_win = None
    for layer in range(DEPTH):
        if layer == n_a:
            kv_cmp, kv_slc, kv_win = build_shared_kv(
                x, c, kv_norm_g, kv_ada_w, kv_ada_b, kv_w, cmp_pos, cmp_w1, cmp_w2)
        mod = (jax.nn.silu(c) @ ada_w[layer] + ada_b[layer]).reshape(B, 3, 3, D)
        g = norm_g[layer]
        h = modulate_pre(x, g[0, 0], mod[:, 0, 0], mod[:, 0, 1])
        y = swiglu_ffn(h, ffn_w_in[layer, 0], ffn_w_out[layer, 0])
        x = gated_post_add(x, y, g[0, 1], mod[:, 0, 2], 0.5)
        h = modulate_pre(x, g[1, 0], mod[:, 1, 0], mod[:, 1, 1])
        if layer < n_a:
            y = short_conv_mixer(h, a_w_in[layer], a_conv[layer], a_w_out[layer])
        else:
            y = nsa_mixer(h, kv_cmp, kv_slc, kv_win, b_w_in[layer - n_a], b_w_out[layer - n_a])
        x = gated_post_add(x, y, g[1, 1], mod[:, 1, 2], 1.0)
        h = modulate_pre(x, g[2, 0], mod[:, 2, 0], mod[:, 2, 1])
        y = swiglu_ffn(h, ffn_w_in[layer, 1], ffn_w_out[layer, 1])
        x = gated_post_add(x, y, g[2, 1], mod[:, 2, 2], 0.5)
    return x
```

```python
import contextlib
import numpy as np
import concourse.bass as bass
import concourse.mybir as mybir
from concourse.bass_utils import run_bass_kernel_spmd

F32 = mybir.dt.float32
BF16 = mybir.dt.bfloat16
AF = mybir.ActivationFunctionType
ALU = mybir.AluOpType
AX = mybir.AxisListType

D = 1024
SEQ = 4096
DFF = 2816
NJ = DFF // 128
NT = 8
TG = NT * 128
NGRP = SEQ // TG
NNG = TG // 512
EPS = 1e-6
NL = 4


class Buf:
    __slots__ = ("name", "w", "r")

    def __init__(self, name):
        self.name = name
        self.w = None
        self.r = []


class Sched:
    def __init__(self, nc, ctx):
        self.nc = nc
        self.ctx = ctx
        self.E = {"pe": nc.tensor, "act": nc.scalar, "dve": nc.vector, "pool": nc.gpsimd, "sp": nc.sync}
        self.semh = {}
        self.cnt = {}
        self.seen = {e: {} for e in self.E}
        self.n_inst = 0
        self.n_wait = 0
        for e in ("pe", "act", "dve", "pool"):
            self.newsem(e)

    def newsem(self, key):
        self.semh[key] = self.ctx.enter_context(self.nc.semaphore("s_" + key))
        self.cnt[key] = 0
        return key

    def _wait(self, eng, tok):
        if tok is None:
            return
        key, val = tok
        if self.seen[eng].get(key, 0) >= val:
            return
        assert val <= self.cnt[key], f"wait on unsignaled token {tok} cnt={self.cnt[key]} from {eng}"
        self.E[eng].wait_ge(self.semh[key], val)
        self.seen[eng][key] = val
        self.n_wait += 1

    def _deps(self, eng, reads, writes):
        for b in reads:
            self._wait(eng, b.w)
        for b in writes:
            if b.w is not None and b.w[0] != eng:
                self._wait(eng, b.w)
            for t in b.r:
                if t[0] != eng:
                    self._wait(eng, t)

    def _commit(self, tok, reads, writes):
        for b in reads:
            b.r.append(tok)
            if len(b.r) > 48:
                best = {}
                for k, v in b.r:
                    if best.get(k, 0) < v:
                        best[k] = v
                b.r = list(best.items())
        for b in writes:
            b.w = tok
            b.r = []

    def op(self, eng, fn, reads=(), writes=(), sig=True):
        self._deps(eng, reads, writes)
        ins = fn(self.E[eng])
        self.n_inst += 1
        if sig:
            ins.then_inc(self.semh[eng], 1)
            self.cnt[eng] += 1
            tok = (eng, self.cnt[eng])
        else:
            tok = (eng, self.cnt[eng] + 1)
        self._commit(tok, reads, writes)
        return tok

    def dma(self, q, out, in_, semkey, reads=(), writes=(), **kw):
        self._deps(q, reads, writes)
        ins = self.E[q].dma_start(out=out, in_=in_, **kw)
        ins.then_inc(self.semh[semkey], 16)
        self.n_inst += 1
        self.cnt[semkey] += 16
        tok = (semkey, self.cnt[semkey])
        self._commit(tok, reads, writes)
        return tok

    def wait_all(self, eng, bufs):
        for b in bufs:
            self._wait(eng, b.w)
            for t in b.r:
                self._wait(eng, t)


def _ktile(w):
    K, N = w.shape
    return np.ascontiguousarray(w.reshape(K // 128, 128, N).transpose(1, 0, 2).reshape(128, (K // 128) * N))


def _colT(v, nchunk):
    return np.ascontiguousarray(v.reshape(nchunk, 128).T)


def _bf16_split(a, n):
    import ml_dtypes
    out = []
    r = np.asarray(a, np.float64)
    for _ in range(n):
        p = r.astype(np.float32).astype(ml_dtypes.bfloat16).astype(np.float32)
        out.append(p)
        r = r - p.astype(np.float64)
    return out


def host_constants():
    c = {}
    c["ident"] = np.eye(128, dtype=np.float32)
    t = np.arange(SEQ)
    p = (t % 128).astype(np.float32)
    kt = (128 * (t // 128)).astype(np.float32)
    one = np.ones(SEQ, np.float32)
    ka = np.stack([p, p, kt, kt, one, one, one])
    ka = ka.reshape(7, SEQ // 128, 128).transpose(1, 0, 2)
    c["kaug"] = np.ascontiguousarray(np.broadcast_to(ka[:, :, None, :], (SEQ // 128, 7, 4, 128)))
    n = np.arange(256)
    a = (31 + 16 * (n % 8)).astype(np.float32)
    bb = (128 * (n // 8)).astype(np.float32)
    o2 = np.ones(256, np.float32)
    kc = np.stack([a, a, bb, bb, o2, o2, o2])
    c["kaugc"] = np.ascontiguousarray(np.broadcast_to(kc[:, None, :], (7, 4, 256)))
    h = np.arange(1, 17, dtype=np.float64)
    slope = np.exp2(-8.0 * h / 16)
    s_hi, s_lo = _bf16_split(slope, 2)
    s_eff = s_hi.astype(np.float64) + s_lo.astype(np.float64)
    cq = -(s_eff[:, None] * t[None, :].astype(np.float64))
    c1, c2, c3 = _bf16_split(cq, 3)
    bc = lambda v: np.broadcast_to(v[:, None], (16, SEQ)).astype(np.float32)
    c["qaug"] = np.ascontiguousarray(np.stack([bc(s_hi), bc(s_lo), bc(s_hi), bc(s_lo), c1, c2, c3]))
    j = np.arange(64)
    c["fmat"] = np.where((t[None, :] // 64) == j[:, None], 1.0, 0.0).astype(np.float32)
    k = np.arange(128)
    c["mbdiag"] = np.where(k[:, None] <= k[None, :], 0.0, -30000.0).astype(np.float32)
    c["mbfar"] = np.where(k[:, None] > k[None, :], 0.0, -30000.0).astype(np.float32)
    c["tcmp"] = np.where(16 * k[:, None] + 31 <= t[None, :], 0.0, -30000.0).astype(np.float32)
    tq = t.reshape(32, 128)
    cur = tq // 64
    jj = j[None, None, :]
    fb = np.zeros((32, 128, 64), np.float32)
    fb = np.where(64 * jj > tq[:, :, None], -1e30, fb)
    fb = np.where(jj == cur[:, :, None] - 1, 1e30, fb)
    fb = np.where(jj == cur[:, :, None], 2e30, fb)
    fb = np.where(jj == 0, 3e30, fb)
    c["fbias"] = np.ascontiguousarray(fb.astype(np.float32))
    nn = np.arange(256)
    ov = ((16 * nn[:, None] < 64 * j[None, :] + 64) & (16 * nn[:, None] + 31 >= 64 * j[None, :])).astype(np.float32)
    c["ovl"] = np.ascontiguousarray(ov.reshape(2, 128, 64).transpose(1, 0, 2))
    return c


def host_percore(inp, b, ntok=SEQ):
    f = {}
    f["x"] = np.ascontiguousarray(inp["x"][b][:ntok])
    f["ccol"] = _colT(np.asarray(inp["c"][b]), 8)
    return f


def host_layout(inp):
    f = {}
    ada_w = inp["ada_w"]
    f["adaw"] = np.ascontiguousarray(
        np.stack([np.stack([_ktile(ada_w[l][:, u * 256:(u + 1) * 256]) for u in range(36)]) for l in range(NL)]))
    ada_b = inp["ada_b"]
    f["adabT"] = np.ascontiguousarray(np.concatenate([_colT(ada_b[l], 72) for l in range(NL)], axis=1))
    g = np.stack([np.stack([ada_b[l].reshape(3, 3, D)[s, 2] for s in range(3)]) for l in range(NL)])
    f["adabg"] = np.ascontiguousarray(np.broadcast_to(g[:, :, None, :], (NL, 3, 128, D)))
    ng = inp["norm_g"]
    f["gpost"] = np.ascontiguousarray(np.broadcast_to(ng[:, :, 1][:, :, None, :], (NL, 3, 128, D)))
    f["gpreT"] = np.ascontiguousarray(
        np.concatenate([_colT(ng[l, s, 0], 8) for l in range(NL) for s in range(3)], axis=1))
    f["kvadaw"] = np.ascontiguousarray(np.stack([_ktile(inp["kv_ada_w"][:, u * 256:(u + 1) * 256]) for u in range(8)]))
    f["kvadabT"] = _colT(inp["kv_ada_b"], 16)
    f["kvgT"] = _colT(inp["kv_norm_g"], 8)
    wi = inp["ffn_w_in"]
    f["ffnin"] = np.ascontiguousarray(np.stack([np.stack([np.stack([
        _ktile(np.concatenate([wi[l, s][:, j * 128:(j + 1) * 128], wi[l, s][:, DFF + j * 128:DFF + (j + 1) * 128]], axis=1))
        for j in range(NJ)]) for s in range(2)]) for l in range(NL)]))
    wo = inp["ffn_w_out"]
    f["ffnout"] = np.ascontiguousarray(np.stack([np.stack([_ktile(wo[l, s]) for s in range(2)]) for l in range(NL)]))
    aw = inp["a_w_in"]
    f["awin"] = np.ascontiguousarray(np.stack([np.stack([
        _ktile(np.concatenate([aw[l][:, k * D + c * 128:k * D + (c + 1) * 128] for k in range(3)], axis=1))
        for c in range(8)]) for l in range(2)]))
    f["awout"] = np.ascontiguousarray(np.stack([_ktile(inp["a_w_out"][l]) for l in range(2)]))
    f["kvw"] = _ktile(inp["kv_w"])
    f["bwin"] = np.ascontiguousarray(np.stack([_ktile(inp["b_w_in"][l]) for l in range(2)]))
    f["bwout"] = np.ascontiguousarray(np.stack([_ktile(inp["b_w_out"][l]) for l in range(2)]))
    f["cmpw1"] = np.ascontiguousarray(inp["cmp_w1"].reshape(2, 32, 64, 256).transpose(2, 0, 1, 3).reshape(64, 16384))
    f["cmpw2"] = np.ascontiguousarray(inp["cmp_w2"].reshape(2, 2, 128, 64).transpose(2, 0, 1, 3).reshape(128, 256))
    f["cmpposT"] = np.ascontiguousarray(inp["cmp_pos"].transpose(2, 0, 1).reshape(64, 64))
    ac = inp["a_conv"]
    f["aconvT"] = np.ascontiguousarray(np.concatenate(
        [np.stack([_colT(ac[l, k], 8) for k in range(3)], axis=2).reshape(128, 24) for l in range(2)], axis=1))
    return f


class Prog:
    def __init__(self, ngrp=NGRP, layers=NL, sub_limit=None):
        self.ngrp = ngrp
        self.layers = layers
        self.sub_limit = sub_limit
        self.nc = bass.Bass("TRN2", target_bir_lowering=False)
        self.din = {}

    def dram_in(self, name, shape):
        t = self.nc.dram_tensor(name, list(shape), F32, kind="ExternalInput")
        self.din[name] = t
        return t.ap()

    def sb(self, name, shape, dt):
        return self.ctx.enter_context(self.nc.sbuf_tensor("sb_" + name, list(shape), dt))

    def ps(self, name, shape, dt):
        return self.ctx.enter_context(self.nc.psum_tensor("ps_" + name, list(shape), dt))

    def build(self, shapes):
        nc = self.nc
        A = {k: self.dram_in(k, v) for k, v in shapes.items()}
        self.A = A
        self.out = nc.dram_tensor("out", [self.ngrp * TG, D], F32, kind="ExternalOutput").ap()
        self.cbscr = nc.dram_tensor("cbscr", [NL * 3, 128, D], F32, kind="Internal").ap()
        self.kscr = nc.dram_tensor("kscr", [2, SEQ // 128, 71, 4, 128], BF16, kind="Internal").ap()
        self.vscr = nc.dram_tensor("vscr", [2, SEQ // 128, 128, 260], BF16, kind="Internal").ap()
        with contextlib.ExitStack() as ctx:
            self.ctx = ctx
            S = self.S = Sched(nc, ctx)
            sb, ps = self.sb, self.ps
            self.xres = sb("xres", [128, NT, D], F32)
            self.hT = sb("hT", [128, 8, TG], BF16)
            self.actT = sb("actT", [128, NJ, TG], BF16)
            self.wbig = sb("wbig", [128, NJ * D], BF16)
            self.ring = sb("ring", [128, 3, 3072], BF16)
            self.cb = sb("cb", [128, 2, D], F32)
            self.xn = sb("xn", [128, 2, D], BF16)
            self.tmpb = sb("tmpb", [128, 2, 512], BF16)
            self.tmp32 = sb("tmp32", [128, D], F32)
            self.junk = sb("junk", [128, D], BF16)
            self.ident = sb("ident", [128, 128], BF16)
            self.st = sb("stat", [128, 64], F32)
            self.modT = sb("modT", [128, 13 * 16], F32)
            self.sc = sb("sc", [128, 8], F32)
            self.scb = sb("scb", [128, 8], BF16)
            self.screp = sb("screp", [128, 8, 128], BF16)
            self.pro = sb("pro", [128, 420], F32)
            self.halo = sb("halo", [128, 32], F32)
            self.convw = sb("convw", [128, 48], F32)
            self.nhalf = sb("nhalf", [128, 8], F32)
            self.kcT = sb("kcT", [128, 4, 256], BF16)
            self.vc = sb("vc", [128, 2, 4, 129], BF16)
            self.hidV = sb("hidV", [128, 2, 4, 256], BF16)
            self.cmphalo = sb("cmphalo", [64, 2, 4, 16], BF16)
            self.biash = sb("biash", [128, 4], F32)
            self.w2 = sb("w2", [128, 256], BF16)
            self.posT = sb("posT", [64, 64], BF16)
            self.vst = sb("vst", [128, 2, 260], BF16)
            self.gates = sb("gates", [128, NT, 48], F32)
            self.imp = sb("imp", [128, 4, 64], F32)
            self.impw = sb("impw", [128, 4, 64], F32)
            self.mx = sb("mx", [128, 16], F32)
            self.mbq = sb("mbq", [128, 4, 64], BF16)
            self.msk = sb("msk", [128, 2, 2, 128], BF16)
            self.fbs = sb("fbs", [128, 2, 64], F32)
            self.mbd = sb("mbd", [128, 2, 128], BF16)
            self.gsm = sb("gsm", [128, 64], F32)
            self.cmpt = sb("cmpt", [128, 8, 64], F32)
            self.hidK = sb("hidK", [128, 2, 64], BF16)
            self.pT = ps("pT", [128, D], BF16)
            self.pA = [ps(f"pA{i}", [128, 512], F32) for i in range(4)]
            self.pY = ps("pY", [128, D], F32)
            self.pM = ps("pM", [128, 512], F32)
            B = lambda n: Buf(n)
            self.b_xres = [B(f"xres{t}") for t in range(NT)]
            self.b_hT = [B(f"hT{t}") for t in range(NT)]
            self.b_actT = [[B(f"actT{j}_{n}") for n in range(NNG)] for j in range(NJ)]
            self.b_wbig = B("wbig")
            self.b_ring = [B(f"ring{i}") for i in range(3)]
            self.b_cb = [B("cb0"), B("cb1")]
            self.b_xn = [B("xn0"), B("xn1")]
            self.b_tmpb = [B("tmpb0"), B("tmpb1")]
            self.b_tmp32 = B("tmp32")
            self.b_junk = B("junk")
            self.b_const = B("const")
            self.b_ident = B("ident")
            self.b_st = B("st")
            self.b_modT = B("modT")
            self.b_pT = B("pT")
            self.b_pA = [B(f"pA{i}") for i in range(4)]
            self.b_pY = B("pY")
            self.b_pM = B("pM")
            self.b_cbscr = [B(f"cbscr{i}") for i in range(NL * 3)]
            self.b_out = B("out")
            self.b_halo = B("halo")
            self.b_kscr = [B("kscr0"), B("kscr1")]
            self.b_vscr = [B("vscr0"), B("vscr1")]
            self.b_kcT = B("kcT"); self.b_vc = B("vc"); self.b_hidV = B("hidV"); self.b_cmph = B("cmph")
            self.b_biash = B("biash"); self.b_c2 = B("c2"); self.b_vst = [B("vst0"), B("vst1")]
            self.b_gates = B("gates"); self.b_imp = B("imp"); self.b_impw = B("impw"); self.b_mx = B("mx")
            self.b_mbq = B("mbq"); self.b_msk = [B("msk0"), B("msk1")]; self.b_gsm = B("gsm")
            self.b_cmpt = B("cmpt"); self.b_hidK = B("hidK")
            for k in ("w1", "c2", "vst0", "vst1", "kst", "msk0", "msk1", "qaug", "fmat", "kt0", "kt1", "kt2", "vt0", "vt1", "vt2"):
                S.newsem(k)
            self.vst_i = 0
            self.msk_i = 0
            self.kv_i = 0
            for i in range(3):
                S.newsem(f"ring{i}")
            for k in ("wbig", "cb0", "cb1", "xin", "xout", "misc", "cbst0", "cbst1", "identl"):
                S.newsem(k)
            self.ring_i = 0
            self.cb_i = 0
            self.xn_i = 0

            self.emit()
            S.wait_all("sp", [self.b_out])
            self.stats = (S.n_inst, S.n_wait)
        return nc

    def ring_load(self, src_ap, ncols):
        i = self.ring_i % 3
        self.ring_i += 1
        dst = self.ring[:, i, 0:ncols]
        self.S.dma("pool", dst, src_ap, f"ring{i}", writes=[self.b_ring[i]])
        return i

    def emit(self):
        S = self.S
        A = self.A
        S.dma("pool", self.ident[:], A["ident"], "identl", writes=[self.b_ident])
        self.prologue()
        subs = []
        for l in range(self.layers):
            subs += [("ffn", l, 0), ("mix", l, 1), ("ffn", l, 2)]
        if self.sub_limit is not None:
            subs = subs[: self.sub_limit]
        for g in range(self.ngrp):
            S.dma("sp", self.xres[:], A["x"][g * TG:(g + 1) * TG, :].rearrange("(t p) d -> p t d", p=128), "xin",
                  writes=self.b_xres)
            for (kind, l, s) in subs:
                if kind == "ffn":
                    self.ffn(l, s)
                elif l < 2:
                    self.mixer_a(l, g)
                else:
                    self.nsa(l, g)
                if kind == "ffn" and l == 1 and s == 2:
                    self.kv_phase(g)
            if not getattr(self, "dbg_o", False):
                S.dma("sp", self.out[g * TG:(g + 1) * TG, :].rearrange("(t p) d -> p t d", p=128), self.xres[:], "xout",
                      reads=self.b_xres, writes=[self.b_out])

    def prologue(self):
        S, A = self.S, self.A
        pro = self.pro
        S.dma("sp", pro[:, 0:288], A["adabT"], "misc", writes=[self.b_const])
        S.dma("sp", pro[:, 288:384], A["gpreT"], "misc", writes=[self.b_const])
        S.dma("sp", pro[:, 384:392], A["ccol"], "misc", writes=[self.b_const])
        S.dma("sp", self.convw[:], A["aconvT"], "misc", writes=[self.b_const])
        S.op("dve", lambda e: e.memset(self.halo[:], 0.0), writes=[self.b_halo])
        S.op("dve", lambda e: e.memset(self.nhalf[:], -0.5), writes=[self.b_halo])
        S.op("dve", lambda e: e.memset(self.nhalf[:, 1:2], EPS), writes=[self.b_halo])
        S.op("act", lambda e: e.activation(out=self.sc[:], in_=pro[:, 384:392], func=AF.Silu),
             reads=[self.b_const], writes=[self.b_st])
        S.op("dve", lambda e: e.tensor_copy(out=self.scb[:], in_=self.sc[:]), reads=[self.b_st], writes=[self.b_modT])
        for kc in range(8):
            S.op("dve", lambda e: e.tensor_copy(out=self.screp[:, kc, :], in_=self.sc[:, kc:kc + 1].to_broadcast([128, 128])),
                 reads=[self.b_st], writes=[self.b_modT])
        if self.layers > 2 or (self.layers == 2 and (self.sub_limit is None or self.sub_limit >= 6)):
            self.nsa_init()
        for l in range(self.layers):
            for u in range(36):
                sub, kind, q = u // 12, (u % 12) // 4, u % 4
                ri = self.ring_load(A["adaw"][l, u], 2048)
                rb = self.b_ring[ri]
                un = self.ring[:, ri, :]
                si = l * 3 + sub
                if kind < 2:
                    self.mod_cols(un, rb, 2, pro[:, l * 72 + u * 2: l * 72 + u * 2 + 2],
                                  si * 16 + (8 if kind == 0 else 0) + q * 2, kind,
                                  pro[:, 288 + si * 8 + q * 2: 288 + si * 8 + q * 2 + 2])
                else:
                    for kc in range(8):
                        S.op("pe", lambda e: e.matmul(self.pA[0][:, 0:256], lhsT=self.screp[:, kc, :], rhs=un[:, kc * 256:(kc + 1) * 256],
                                                      start=(kc == 0), stop=(kc == 7)),
                             reads=[rb, self.b_modT], writes=[self.b_pA[0]], sig=(kc == 7))
                    ci = self.cb_i % 2
                    self.cb_i += 1
                    cbv = self.cb[:, ci, :]
                    bcb = self.b_cb[ci]
                    S.dma("sp", cbv[:, 0:256], A["adabg"][l, sub, :, q * 256:(q + 1) * 256], f"cb{ci}", writes=[bcb])
                    S.dma("sp", cbv[:, 256:512], A["gpost"][l, sub, :, q * 256:(q + 1) * 256], f"cb{ci}", writes=[bcb])
                    wgt = 1.0 if sub == 1 else 0.5
                    S.op("dve", lambda e: e.tensor_tensor(out=cbv[:, 0:256], in0=self.pA[0][:, 0:256], in1=cbv[:, 0:256], op=ALU.add),
                         reads=[self.b_pA[0], bcb], writes=[bcb])
                    S.op("dve", lambda e: e.scalar_tensor_tensor(out=cbv[:, 0:256], in0=cbv[:, 0:256], scalar=wgt, in1=cbv[:, 256:512],
                                                                 op0=ALU.mult, op1=ALU.mult),
                         reads=[bcb], writes=[bcb])
                    S.dma("sp", self.cbscr[si, :, q * 256:(q + 1) * 256], cbv[:, 0:256], f"cbst{ci}",
                          reads=[bcb], writes=[self.b_cbscr[si]])

    def nsa_init(self):
        S, A = self.S, self.A
        pro = self.pro
        S.dma("sp", pro[:, 392:408], A["kvadabT"], "misc", writes=[self.b_const])
        S.dma("sp", pro[:, 408:416], A["kvgT"], "misc", writes=[self.b_const])
        S.dma("pool", self.w2[:], A["cmpw2"], "c2", writes=[self.b_c2])
        S.dma("pool", self.posT[:], A["cmpposT"], "c2", writes=[self.b_c2])
        S.dma("pool", self.mbd[:, 0, :], A["mbdiag"], "c2", writes=[self.b_c2])
        S.dma("pool", self.mbd[:, 1, :], A["mbfar"], "c2", writes=[self.b_c2])
        S.dma("pool", self.kcT[64:71, :, :], A["kaugc"], "c2", writes=[self.b_c2])
        for nt in range(2):
            for gg in range(4):
                S.dma("pool", self.vc[:, nt, gg, 65:129], A["ovl"][:, nt, :], "c2", writes=[self.b_c2])
        for br in range(2):
            S.dma("pool", self.kscr[br, :, 64:71, :, :], A["kaug"], "c2", writes=[self.b_c2])
        S.op("dve", lambda e: e.memset(self.kcT[0:64, :, :], 0.0), reads=[self.b_c2], writes=[self.b_kcT])
        S.op("dve", lambda e: e.memset(self.vc[:, :, :, 0:64], 0.0), reads=[self.b_c2], writes=[self.b_vc])
        S.op("dve", lambda e: e.memset(self.vc[:, :, :, 64:65], 1.0), reads=[self.b_c2], writes=[self.b_vc])
        S.op("dve", lambda e: e.memset(self.hidV[:], 0.0), writes=[self.b_hidV])
        S.op("dve", lambda e: e.memset(self.cmphalo[:], 0.0), writes=[self.b_cmph])
        S.op("dve", lambda e: e.memset(self.vst[:], 1.0), writes=self.b_vst)
        for u in range(8):
            ri = self.ring_load(A["kvadaw"][u], 2048)
            kind, q = (0 if u < 4 else 1), u % 4
            self.mod_cols(self.ring[:, ri, :], self.b_ring[ri], 2, pro[:, 392 + u * 2: 392 + u * 2 + 2],
                          12 * 16 + (8 if kind == 0 else 0) + q * 2, kind, pro[:, 408 + q * 2: 408 + q * 2 + 2])

    def mod_cols(self, un, rb, ncc, bias_ap, dst0, kind, g_ap):
        S = self.S
        for cc in range(ncc):
            for kc in range(8):
                S.op("pe", lambda e: e.matmul(self.pM[:, cc:cc + 1], lhsT=un[:, kc * 256 + cc * 128: kc * 256 + (cc + 1) * 128],
                                              rhs=self.scb[:, kc:kc + 1], start=(kc == 0), stop=(kc == 7)),
                     reads=[rb, self.b_modT], writes=[self.b_pM], sig=(kc == 7 and cc == ncc - 1))
        dst = self.modT[:, dst0:dst0 + ncc]
        if kind == 0:
            S.op("dve", lambda e: e.tensor_tensor(out=dst, in0=self.pM[:, 0:ncc], in1=bias_ap, op=ALU.add),
                 reads=[self.b_pM, self.b_const], writes=[self.b_modT])
        else:
            S.op("dve", lambda e: e.scalar_tensor_tensor(out=dst, in0=self.pM[:, 0:ncc], scalar=1.0, in1=bias_ap, op0=ALU.add, op1=ALU.add),
                 reads=[self.b_pM, self.b_const], writes=[self.b_modT])
            S.op("dve", lambda e: e.tensor_tensor(out=dst, in0=dst, in1=g_ap, op=ALU.mult),
                 reads=[self.b_modT, self.b_const], writes=[self.b_modT])

    def norm_tile(self, t, si):
        S = self.S
        x = self.xres[:, t, :]
        st = self.st
        S.op("act", lambda e: e.activation(out=self.junk[:], in_=x, func=AF.Square, accum_out=st[:, t:t + 1]),
             reads=[self.b_xres[t]], writes=[self.b_junk, self.b_st])
        S.op("act", lambda e: e.activation(out=st[:, 8 + t:9 + t], in_=st[:, t:t + 1], func=AF.Sqrt, scale=1.0 / D, bias=self.nhalf[:, 1:2]),
             reads=[self.b_st, self.b_halo], writes=[self.b_st])
        S.op("dve", lambda e: e.reciprocal(out=st[:, 16 + t:17 + t], in_=st[:, 8 + t:9 + t]), reads=[self.b_st], writes=[self.b_st])
        xi = self.xn_i % 2
        self.xn_i += 1
        xn = self.xn[:, xi, :]
        S.op("act", lambda e: e.activation(out=xn, in_=x, func=AF.Copy, scale=st[:, 16 + t:17 + t]),
             reads=[self.b_xres[t], self.b_st], writes=[self.b_xn[xi]])
        for c in range(8):
            S.op("pe", lambda e: e.transpose(out=self.pT[:, c * 128:(c + 1) * 128], in_=xn[:, c * 128:(c + 1) * 128],
                                             identity=self.ident[:]),
                 reads=[self.b_xn[xi], self.b_ident], writes=[self.b_pT], sig=(c == 7))
        for c in range(8):
            a_col = self.modT[:, si * 16 + c: si * 16 + c + 1]
            s_col = self.modT[:, si * 16 + 8 + c: si * 16 + 8 + c + 1]
            dst = self.hT[:, c, t * 128:(t + 1) * 128]
            src = self.pT[:, c * 128:(c + 1) * 128]
            if c % 2 == 0:
                S.op("act", lambda e: e.activation(out=dst, in_=src, func=AF.Identity, bias=s_col, scale=a_col),
                     reads=[self.b_pT, self.b_modT], writes=[self.b_hT[t]])
            else:
                S.op("dve", lambda e: e.tensor_scalar(out=dst, in0=src, scalar1=a_col, scalar2=s_col, op0=ALU.mult, op1=ALU.add),
                     reads=[self.b_pT, self.b_modT], writes=[self.b_hT[t]])

    def ybank(self, t):
        if t % 2 == 0:
            return self.pY[:, 0:512], self.pY[:, 512:1024], [self.b_pY]
        return self.pA[0][:], self.pA[1][:], [self.b_pA[0], self.b_pA[1]]

    def post_tile(self, t, cbv, bcb):
        S = self.S
        st = self.st
        y0, y1, by = self.ybank(t)
        S.op("act", lambda e: e.activation(out=self.junk[:, 0:512], in_=y0, func=AF.Square, accum_out=st[:, 24 + t:25 + t]),
             reads=by, writes=[self.b_junk, self.b_st])
        S.op("act", lambda e: e.activation(out=self.junk[:, 512:1024], in_=y1, func=AF.Square, accum_out=st[:, 48 + t:49 + t]),
             reads=by, writes=[self.b_junk, self.b_st])
        S.op("dve", lambda e: e.tensor_tensor(out=st[:, 32 + t:33 + t], in0=st[:, 24 + t:25 + t], in1=st[:, 48 + t:49 + t], op=ALU.add),
             reads=[self.b_st], writes=[self.b_st])
        S.op("act", lambda e: e.activation(out=st[:, 56 + t:57 + t], in_=st[:, 32 + t:33 + t], func=AF.Sqrt, scale=1.0 / D, bias=self.nhalf[:, 1:2]),
             reads=[self.b_st, self.b_halo], writes=[self.b_st])
        S.op("dve", lambda e: e.reciprocal(out=st[:, 40 + t:41 + t], in_=st[:, 56 + t:57 + t]), reads=[self.b_st], writes=[self.b_st])
        S.op("dve", lambda e: e.scalar_tensor_tensor(out=self.tmp32[:, 0:512], in0=y0, scalar=st[:, 40 + t:41 + t], in1=cbv[:, 0:512],
                                                     op0=ALU.mult, op1=ALU.mult),
             reads=by + [self.b_st, bcb], writes=[self.b_tmp32])
        S.op("dve", lambda e: e.scalar_tensor_tensor(out=self.tmp32[:, 512:1024], in0=y1, scalar=st[:, 40 + t:41 + t], in1=cbv[:, 512:1024],
                                                     op0=ALU.mult, op1=ALU.mult),
             reads=by + [self.b_st, bcb], writes=[self.b_tmp32])
        S.op("dve", lambda e: e.tensor_tensor(out=self.xres[:, t, :], in0=self.xres[:, t, :], in1=self.tmp32[:], op=ALU.add),
             reads=[self.b_tmp32, self.b_xres[t]], writes=[self.b_xres[t]])

    def load_cb(self, si):
        S = self.S
        ci = self.cb_i % 2
        self.cb_i += 1
        S.dma("sp", self.cb[:, ci, :], self.cbscr[si], f"cb{ci}", reads=[self.b_cbscr[si]], writes=[self.b_cb[ci]])
        return self.cb[:, ci, :], self.b_cb[ci]

    def load_wbig(self, src, ncols, nsplit=4):
        S = self.S
        step = ncols // nsplit
        for i in range(nsplit):
            S.dma("pool", self.wbig[:, i * step:(i + 1) * step], src[:, i * step:(i + 1) * step], "wbig", writes=[self.b_wbig])

    def ffn(self, l, s):
        S, A = self.S, self.A
        si = l * 3 + s
        fs = 0 if s == 0 else 1
        cbv, bcb = self.load_cb(si)
        pre = [self.ring_load(A["ffnin"][l, fs, j], 2048) for j in range(2)]
        for t in range(NT):
            self.norm_tile(t, si)
        slots = list(pre)
        for j in range(NJ):
            if j + 2 < NJ:
                slots.append(self.ring_load(A["ffnin"][l, fs, j + 2], 2048))
            S.dma("pool", self.wbig[:, j * D:(j + 1) * D], A["ffnout"][l, fs][:, j * D:(j + 1) * D], "wbig", writes=[self.b_wbig])
            ri = slots[j]
            un = self.ring[:, ri, :]
            rb = self.b_ring[ri]
            for n in range(NNG):
                pg, pu = (self.pA[0], self.pA[1]) if (j * NNG + n) % 2 == 0 else (self.pA[2], self.pA[3])
                bg, bu = (self.b_pA[0], self.b_pA[1]) if (j * NNG + n) % 2 == 0 else (self.b_pA[2], self.b_pA[3])
                hb = self.b_hT[n * 4:(n + 1) * 4]
                for kc in range(8):
                    S.op("pe", lambda e: e.matmul(pg[:], lhsT=un[:, kc * 256: kc * 256 + 128], rhs=self.hT[:, kc, n * 512:(n + 1) * 512],
                                                  start=(kc == 0), stop=(kc == 7)), reads=[rb] + hb, writes=[bg], sig=(kc == 7))
                for kc in range(8):
                    S.op("pe", lambda e: e.matmul(pu[:], lhsT=un[:, kc * 256 + 128: kc * 256 + 256], rhs=self.hT[:, kc, n * 512:(n + 1) * 512],
                                                  start=(kc == 0), stop=(kc == 7)), reads=[rb] + hb, writes=[bu], sig=(kc == 7))
                ti = (j * NNG + n) % 2
                tb = self.tmpb[:, ti, :]
                S.op("act", lambda e: e.activation(out=tb, in_=pg[:], func=AF.Silu), reads=[bg], writes=[self.b_tmpb[ti]])
                S.op("dve", lambda e: e.tensor_tensor(out=self.actT[:, j, n * 512:(n + 1) * 512], in0=tb, in1=pu[:], op=ALU.mult),
                     reads=[self.b_tmpb[ti], bu], writes=[self.b_actT[j][n]])
        for t in range(NT):
            n = t // 4
            yb = self.ybank(t)
            for dh in range(2):
                for j in range(NJ):
                    S.op("pe", lambda e: e.matmul(yb[dh], lhsT=self.actT[:, j, t * 128:(t + 1) * 128],
                                                  rhs=self.wbig[:, j * D + dh * 512: j * D + (dh + 1) * 512],
                                                  start=(j == 0), stop=(j == NJ - 1)),
                         reads=[self.b_actT[j][n], self.b_wbig], writes=yb[2], sig=(j == NJ - 1))
            self.post_tile(t, cbv, bcb)

    def mixer_a(self, l, g):
        S, A = self.S, self.A
        si = l * 3 + 1
        cbv, bcb = self.load_cb(si)
        pre = [self.ring_load(A["awin"][l, c], 3072) for c in range(2)]
        self.load_wbig(A["awout"][l], 8 * D)
        for t in range(NT):
            self.norm_tile(t, si)
        aT = self.actT
        fl = lambda j0, j1: aT[:, j0:j1, :].rearrange("p a b -> p (a b)")
        v32 = fl(8, 11).bitcast(F32)[:, 0:TG + 2]
        acc = fl(11, 13).bitcast(F32)[:, 0:TG]
        cgs = fl(13, 14).bitcast(F32)[:, 0:512]
        bs = fl(14, 15)[:, 0:TG]
        vb = lambda j0, j1: [self.b_actT[j][n] for j in range(j0, j1) for n in range(NNG)]
        b_v32, b_acc, b_cgs, b_bs = vb(8, 11), vb(11, 13), vb(13, 14), vb(14, 15)
        slots = list(pre)
        for c in range(8):
            if c + 2 < 8:
                slots.append(self.ring_load(A["awin"][l, c + 2], 3072))
            ri = slots[c]
            un = self.ring[:, ri, :]
            rb = self.b_ring[ri]
            for n in range(NNG):
                hb = self.b_hT[n * 4:(n + 1) * 4]
                for k in range(3):
                    for kc in range(8):
                        S.op("pe", lambda e: e.matmul(self.pA[k][:], lhsT=un[:, kc * 384 + k * 128: kc * 384 + (k + 1) * 128],
                                                      rhs=self.hT[:, kc, n * 512:(n + 1) * 512], start=(kc == 0), stop=(kc == 7)),
                             reads=[rb] + hb, writes=[self.b_pA[k]], sig=(kc == 7))
                S.op("act", lambda e: e.activation(out=cgs, in_=self.pA[1][:], func=AF.Copy), reads=[self.b_pA[1]], writes=b_cgs)
                S.op("act", lambda e: e.activation(out=bs[:, n * 512:(n + 1) * 512], in_=self.pA[0][:], func=AF.Copy),
                     reads=[self.b_pA[0]], writes=b_bs)
                S.op("dve", lambda e: e.tensor_tensor(out=v32[:, 2 + n * 512: 2 + (n + 1) * 512], in0=cgs, in1=self.pA[2][:], op=ALU.mult),
                     reads=b_cgs + [self.b_pA[2]], writes=b_v32)
            hc = (l * 8 + c) * 2
            wc = (l * 8 + c) * 3
            cw = lambda k: self.convw[:, wc + k: wc + k + 1]
            S.op("dve", lambda e: e.tensor_copy(out=v32[:, 0:2], in_=self.halo[:, hc:hc + 2]), reads=[self.b_halo], writes=b_v32)
            S.op("dve", lambda e: e.tensor_scalar(out=acc, in0=v32[:, 2:2 + TG], scalar1=cw(2), scalar2=None, op0=ALU.mult),
                 reads=b_v32 + [self.b_const], writes=b_acc)
            S.op("dve", lambda e: e.scalar_tensor_tensor(out=acc, in0=v32[:, 1:1 + TG], scalar=cw(1), in1=acc, op0=ALU.mult, op1=ALU.add),
                 reads=b_v32 + b_acc + [self.b_const], writes=b_acc)
            S.op("dve", lambda e: e.scalar_tensor_tensor(out=acc, in0=v32[:, 0:TG], scalar=cw(0), in1=acc, op0=ALU.mult, op1=ALU.add),
                 reads=b_v32 + b_acc + [self.b_const], writes=b_acc)
            S.op("dve", lambda e: e.tensor_copy(out=self.halo[:, hc:hc + 2], in_=v32[:, TG:TG + 2]), reads=b_v32, writes=[self.b_halo])
            S.op("dve", lambda e: e.tensor_tensor(out=aT[:, c, :], in0=bs, in1=acc, op=ALU.mult),
                 reads=b_bs + b_acc, writes=self.b_actT[c])
        for t in range(NT):
            n = t // 4
            yb = self.ybank(t)
            for dh in range(2):
                for kc in range(8):
                    S.op("pe", lambda e: e.matmul(yb[dh], lhsT=aT[:, kc, t * 128:(t + 1) * 128],
                                                  rhs=self.wbig[:, kc * D + dh * 512: kc * D + (dh + 1) * 512],
                                                  start=(kc == 0), stop=(kc == 7)),
                         reads=[self.b_actT[kc][n], self.b_wbig], writes=yb[2], sig=(kc == 7))
            self.post_tile(t, cbv, bcb)


    def kv_phase(self, g):
        S, A = self.S, self.A
        si = 12
        self.load_wbig(A["kvw"], 8 * 1536, nsplit=2)
        allact = lambda j0, j1: [self.b_actT[j][n] for j in range(j0, j1) for n in range(NNG)]
        w1v = self.actT[0:64, 0:16, :].rearrange("p a b -> p (a b)")
        b_w1 = allact(0, 16)
        for i in range(2):
            S.dma("pool", w1v[:, i * 8192:(i + 1) * 8192], A["cmpw1"][:, i * 8192:(i + 1) * 8192], "w1", writes=b_w1)
        kst = self.actT[0:64, 16:20, :]
        b_kst = allact(16, 20)
        for t in range(NT):
            self.norm_tile(t, si)
        stg = self.ring[0:64, :, :].rearrange("p a b -> p (a b)")[:, 0:8 * (TG + 16)].rearrange("p (k g c) -> p k g c", k=2, g=4)
        b_stg = self.b_ring
        S.op("dve", lambda e: e.tensor_copy(out=stg[:, :, :, 0:16], in_=self.cmphalo[:]), reads=[self.b_cmph], writes=b_stg)
        pi = 0
        for kind in range(4):
            for gg in range(4):
                col0 = [0, 256, 512, 1024][kind] + gg * 64
                for n in range(NNG):
                    pp, bp = self.pA[pi % 4], self.b_pA[pi % 4]
                    pi += 1
                    for kc in range(8):
                        S.op("pe", lambda e: e.matmul(pp[0:64, :], lhsT=self.wbig[:, kc * 1536 + col0: kc * 1536 + col0 + 64],
                                                      rhs=self.hT[:, kc, n * 512:(n + 1) * 512], start=(kc == 0), stop=(kc == 7)),
                             reads=[self.b_wbig] + self.b_hT[n * 4:(n + 1) * 4], writes=[bp], sig=(kc == 7))
                    if kind < 2:
                        dst, bd = stg[:, kind, gg, 16 + n * 512: 16 + (n + 1) * 512], b_stg
                    else:
                        dst, bd = kst[:, gg, n * 512:(n + 1) * 512], b_kst
                    if pi % 2 == 0:
                        S.op("act", lambda e: e.activation(out=dst, in_=pp[0:64, :], func=AF.Copy), reads=[bp], writes=bd)
                    else:
                        S.op("dve", lambda e: e.tensor_copy(out=dst, in_=pp[0:64, :]), reads=[bp], writes=bd)
            if kind >= 2:
                br = kind - 2
                S.dma("sp", self.kscr[br, g * NT:(g + 1) * NT, 0:64, :, :].rearrange("t p g k -> p g t k"),
                      kst.rearrange("p g (t k) -> p g t k", t=NT), "kst", reads=b_kst, writes=[self.b_kscr[br]])
        for br in range(2):
            for t in range(NT):
                for kc in range(8):
                    S.op("pe", lambda e: e.matmul(self.pM[:, 0:256], lhsT=self.hT[:, kc, t * 128:(t + 1) * 128],
                                                  rhs=self.wbig[:, kc * 1536 + (br + 1) * 512 + 256: kc * 1536 + (br + 1) * 512 + 512],
                                                  start=(kc == 0), stop=(kc == 7)),
                         reads=[self.b_wbig, self.b_hT[t]], writes=[self.b_pM], sig=(kc == 7))
                vi = self.vst_i % 2
                self.vst_i += 1
                dstv = self.vst[:, vi, :].rearrange("p (g c) -> p g c", c=65)[:, :, 0:64]
                S.op("act", lambda e: e.activation(out=dstv, in_=self.pM[:, 0:256].rearrange("p (g c) -> p g c", c=64), func=AF.Copy),
                     reads=[self.b_pM], writes=[self.b_vst[vi]])
                S.dma("sp", self.vscr[br, g * NT + t], self.vst[:, vi, :], f"vst{vi}", reads=[self.b_vst[vi]], writes=[self.b_vscr[br]])
        if g == 0:
            for kv in range(2):
                for hc in range(2):
                    for l in range(32):
                        c0 = (kv * 32 + l) * 256 + hc * 128
                        S.op("pe", lambda e: e.matmul(self.pM[:, 256 + kv * 2 + hc: 257 + kv * 2 + hc], lhsT=w1v[:, c0:c0 + 128],
                                                      rhs=self.posT[:, kv * 32 + l: kv * 32 + l + 1], start=(l == 0), stop=(l == 31)),
                             reads=b_w1 + [self.b_c2], writes=[self.b_pM], sig=(l == 31 and kv == 1 and hc == 1))
            S.op("dve", lambda e: e.tensor_copy(out=self.biash[:], in_=self.pM[:, 256:260]), reads=[self.b_pM], writes=[self.b_biash])
        n0 = max(0, 64 * g - 1)
        n1 = 64 * (g + 1) - 2
        nblk = n1 - n0 + 1
        colb = 16 * n0 - g * TG + 16
        ct = self.cmpt
        for kv in range(2):
            for gg in range(4):
                for hc in range(2):
                    pp, bp = self.pA[pi % 4], self.b_pA[pi % 4]
                    pi += 1
                    for l in range(32):
                        c0 = (kv * 32 + l) * 256 + hc * 128
                        S.op("pe", lambda e: e.matmul(pp[:, 0:nblk], lhsT=w1v[:, c0:c0 + 128],
                                                      rhs=stg[:, kv, gg, colb + l: colb + l + 16 * (nblk - 1) + 1: 16],
                                                      start=(l == 0), stop=(l == 31)),
                             reads=b_w1 + b_stg, writes=[bp], sig=(l == 31))
                    xs, x2, u, th, xh = ct[:, 0, 0:nblk], ct[:, 1, 0:nblk], ct[:, 2, 0:nblk], ct[:, 3, 0:nblk], ct[:, 4, 0:nblk]
                    bcol = self.biash[:, kv * 2 + hc: kv * 2 + hc + 1]
                    bc_ = [self.b_cmpt]
                    S.op("dve", lambda e: e.tensor_scalar(out=xs, in0=pp[:, 0:nblk], scalar1=bcol, scalar2=None, op0=ALU.add),
                         reads=[bp, self.b_biash], writes=bc_)
                    S.op("dve", lambda e: e.tensor_tensor(out=x2, in0=xs, in1=xs, op=ALU.mult), reads=bc_, writes=bc_)
                    S.op("dve", lambda e: e.tensor_scalar(out=u, in0=x2, scalar1=0.044715, scalar2=1.0, op0=ALU.mult, op1=ALU.add),
                         reads=bc_, writes=bc_)
                    S.op("dve", lambda e: e.tensor_tensor(out=u, in0=u, in1=xs, op=ALU.mult), reads=bc_, writes=bc_)
                    S.op("act", lambda e: e.activation(out=th, in_=u, func=AF.Tanh, scale=0.7978845608028654), reads=bc_, writes=bc_)
                    S.op("dve", lambda e: e.tensor_scalar(out=xh, in0=xs, scalar1=0.5, scalar2=None, op0=ALU.mult), reads=bc_, writes=bc_)
                    if kv == 0:
                        hdst, bh = self.hidK[:, hc, 0:nblk], [self.b_hidK]
                    else:
                        hdst, bh = self.hidV[:, hc, gg, n0:n0 + nblk], [self.b_hidV]
                    S.op("dve", lambda e: e.scalar_tensor_tensor(out=hdst, in0=th, scalar=1.0, in1=xh, op0=ALU.add, op1=ALU.mult),
                         reads=bc_, writes=bh)
                if kv == 0:
                    for hc in range(2):
                        S.op("pe", lambda e: e.matmul(self.pM[0:64, 0:nblk], lhsT=self.w2[:, hc * 64:(hc + 1) * 64],
                                                      rhs=self.hidK[:, hc, 0:nblk], start=(hc == 0), stop=(hc == 1)),
                             reads=[self.b_c2, self.b_hidK], writes=[self.b_pM], sig=(hc == 1))
                    S.op("act", lambda e: e.activation(out=self.kcT[0:64, gg, n0:n0 + nblk], in_=self.pM[0:64, 0:nblk], func=AF.Copy),
                         reads=[self.b_pM], writes=[self.b_kcT])
                else:
                    for nt in range(n0 // 128, n1 // 128 + 1):
                        for hc in range(2):
                            S.op("pe", lambda e: e.matmul(self.pM[:, 0:64], lhsT=self.hidV[:, hc, gg, nt * 128:(nt + 1) * 128],
                                                          rhs=self.w2[:, 128 + hc * 64: 128 + (hc + 1) * 64], start=(hc == 0), stop=(hc == 1)),
                                 reads=[self.b_c2, self.b_hidV], writes=[self.b_pM], sig=(hc == 1))
                        S.op("act", lambda e: e.activation(out=self.vc[:, nt, gg, 0:64], in_=self.pM[:, 0:64], func=AF.Copy),
                             reads=[self.b_pM], writes=[self.b_vc])
        S.op("dve", lambda e: e.tensor_copy(out=self.cmphalo[:], in_=stg[:, :, :, TG:TG + 16]), reads=b_stg, writes=[self.b_cmph])

    def kv_tile_load(self, br, kt, kview, vview):
        S = self.S
        i = self.kv_i % 3
        self.kv_i += 1
        kdst = kview[:, i]
        vdst = vview[:, i]
        bk, bv = self.b_kt[i], self.b_vt[i]
        S.dma("sp", kdst, self.kscr[br, kt], f"kt{i}", reads=[self.b_kscr[br], self.b_c2], writes=[bk])
        S.dma("sp", vdst, self.vscr[br, kt], f"vt{i}", reads=[self.b_vscr[br]], writes=[bv])
        return kdst, vdst, bk, bv

    def nsa(self, l, g):
        S, A = self.S, self.A
        lb = l - 2
        si = l * 3 + 1
        cbv, bcb = self.load_cb(si)
        S.dma("pool", self.wbig[:, 0:8576], A["bwin"][lb], "wbig", writes=[self.b_wbig])
        S.dma("pool", self.wbig[:, 8576:8576 + 8192], A["bwout"][lb], "wbig", writes=[self.b_wbig])
        allact = lambda j0, j1: [self.b_actT[j][n] for j in range(j0, j1) for n in range(NNG)]
        aflat = self.actT[:, :, :].rearrange("p a b -> p (a b)")
        QT = aflat[:, 0:16 * TG].rearrange("p (h t) -> p h t", h=16)
        b_QT = allact(0, 16)
        fm = aflat[0:64, 16 * TG:16 * TG + SEQ]
        b_fm = allact(16, 20)
        rflat = self.ring[:, :, :].rearrange("p a b -> p (a b)")
        kview = rflat[0:71, 0:1536].rearrange("p (s g k) -> p s g k", s=3, g=4)
        vview = rflat[:, 1536:1536 + 864].rearrange("p (s c) -> p s c", s=3)[:, :, 0:260]
        pview = rflat[:, 2432:2432 + 1536].rearrange("p (s c) -> p s c", s=3)
        MBT = rflat[0:64, 4032:4032 + 512].rearrange("p (g q) -> p g q", g=4)
        m01v = rflat[:, 4544:4544 + 1536].rearrange("p (s c) -> p s c", s=3)
        if not hasattr(self, "b_m01"):
            self.b_m01 = [Buf("m01_0"), Buf("m01_1"), Buf("m01_2")]
            self.m01_i = 0
        if not hasattr(self, "b_kt"):
            self.b_kt = [Buf(f"kt{i}") for i in range(3)]
            self.b_vt = [Buf(f"vt{i}") for i in range(3)]
            self.b_pt = [Buf(f"pt{i}") for i in range(3)]
            self.b_mbt = Buf("mbt")
            self.pt_i = 0
        for b in (self.b_kt + self.b_vt + self.b_pt + [self.b_mbt]):
            pass
        ring_guard = self.b_ring
        for eng in ("sp", "act", "dve", "pool"):
            S.wait_all(eng, self.b_ring)
        for t in range(NT):
            self.norm_tile(t, si)
        S.dma("pool", fm, A["fmat"], "fmat", writes=b_fm)
        S.dma("pool", QT[64:71, :, :], A["qaug"][:, :, g * TG:(g + 1) * TG], "qaug", writes=b_QT)
        pi = 0
        for hh in range(16):
            for n in range(NNG):
                pp, bp = self.pA[pi % 4], self.b_pA[pi % 4]
                pi += 1
                for kc in range(8):
                    S.op("pe", lambda e: e.matmul(pp[0:64, :], lhsT=self.wbig[:, kc * 1072 + hh * 64: kc * 1072 + hh * 64 + 64],
                                                  rhs=self.hT[:, kc, n * 512:(n + 1) * 512], start=(kc == 0), stop=(kc == 7)),
                         reads=[self.b_wbig] + self.b_hT[n * 4:(n + 1) * 4], writes=[bp], sig=(kc == 7))
                dst = QT[0:64, hh, n * 512:(n + 1) * 512]
                if pi % 2 == 0:
                    S.op("act", lambda e: e.activation(out=dst, in_=pp[0:64, :], func=AF.Copy, scale=0.125), reads=[bp], writes=b_QT)
                else:
                    S.op("dve", lambda e: e.tensor_scalar(out=dst, in0=pp[0:64, :], scalar1=0.125, scalar2=None, op0=ALU.mult),
                         reads=[bp], writes=b_QT)
        for t in range(NT):
            for kc in range(8):
                S.op("pe", lambda e: e.matmul(self.pM[:, 0:48], lhsT=self.hT[:, kc, t * 128:(t + 1) * 128],
                                              rhs=self.wbig[:, kc * 1072 + 1024: kc * 1072 + 1072], start=(kc == 0), stop=(kc == 7)),
                     reads=[self.b_wbig, self.b_hT[t]], writes=[self.b_pM], sig=(kc == 7))
            S.op("act", lambda e: e.activation(out=self.gates[:, t, :], in_=self.pM[:, 0:48], func=AF.Tanh, scale=0.5),
                 reads=[self.b_pM], writes=[self.b_gates])
            S.op("dve", lambda e: e.tensor_scalar(out=self.gates[:, t, :], in0=self.gates[:, t, :], scalar1=0.5, scalar2=0.5,
                                                  op0=ALU.mult, op1=ALU.add), reads=[self.b_gates], writes=[self.b_gates])
        o = self.tmp32
        bo = [self.b_tmp32]
        gs = self.gsm
        for t in range(NT):
            T = g * NT + t
            q0 = T * 128
            qs = slice(t * 128, (t + 1) * 128)
            mi = self.msk_i % 2
            self.msk_i += 1
            nts = [0] if T < 16 else [0, 1]
            for nt in nts:
                S.dma("pool", self.msk[:, mi, nt, :], A["tcmp"][:, q0 - 2048 * nt: q0 - 2048 * nt + 128], f"msk{mi}", writes=[self.b_msk[mi]])
            S.dma("pool", self.fbs[:, mi, :], A["fbias"][T], f"msk{mi}", writes=[self.b_msk[mi]])

            def bc4(ap2d):
                return ap2d.rearrange("p (o q) -> p o q", o=1).to_broadcast([ap2d.shape[0], 4, 128])

            def post_cmp(gg):
                for hb in range(2):
                    av = self.pY[:, hb * 512: hb * 512 + 258].rearrange("p (r c) -> p r c", c=129)
                    rd = gs[:, 0:2].rearrange("p (r o) -> p r o", o=1)
                    gr = gs[:, 2:4].rearrange("p (r o) -> p r o", o=1)
                    gcol = self.gates[:, t, gg * 4 + hb * 2: gg * 4 + hb * 2 + 2].rearrange("p (r o) -> p r o", o=1)
                    S.op("dve", lambda e: e.tensor_scalar(out=rd, in0=av[:, :, 64:65], scalar1=1e-30, scalar2=None, op0=ALU.max),
                         reads=[self.b_pY], writes=[self.b_gsm])
                    S.op("dve", lambda e: e.reciprocal(out=rd, in_=rd), reads=[self.b_gsm], writes=[self.b_gsm])
                    S.op("dve", lambda e: e.tensor_tensor(out=gr, in0=rd, in1=gcol, op=ALU.mult), reads=[self.b_gsm, self.b_gates], writes=[self.b_gsm])
                    oc = o[:, (gg * 4 + hb * 2) * 64:(gg * 4 + hb * 2 + 2) * 64].rearrange("p (r c) -> p r c", c=64)
                    S.op("dve", lambda e: e.tensor_tensor(out=oc, in0=av[:, :, 0:64], in1=gr.to_broadcast([128, 2, 64]), op=ALU.mult),
                         reads=[self.b_pY, self.b_gsm], writes=bo)
                    iw = self.impw[:, hb * 2:hb * 2 + 2, :]
                    S.op("dve", lambda e: e.tensor_tensor(out=iw, in0=av[:, :, 65:129], in1=rd.to_broadcast([128, 2, 64]), op=ALU.mult),
                         reads=[self.b_pY, self.b_gsm], writes=[self.b_impw])
                ig = self.imp[:, gg, :]
                S.op("dve", lambda e: e.tensor_tensor(out=ig, in0=self.impw[:, 0, :], in1=self.impw[:, 1, :], op=ALU.add),
                     reads=[self.b_impw], writes=[self.b_imp])
                S.op("dve", lambda e: e.tensor_tensor(out=ig, in0=ig, in1=self.impw[:, 2, :], op=ALU.add), reads=[self.b_impw, self.b_imp], writes=[self.b_imp])
                S.op("dve", lambda e: e.tensor_tensor(out=ig, in0=ig, in1=self.impw[:, 3, :], op=ALU.add), reads=[self.b_impw, self.b_imp], writes=[self.b_imp])
                S.op("dve", lambda e: e.tensor_tensor(out=ig, in0=ig, in1=self.fbs[:, mi, :], op=ALU.add), reads=[self.b_imp, self.b_msk[mi]], writes=[self.b_imp])
                S.op("dve", lambda e: e.max(out=self.mx[:, 0:8], in_=ig), reads=[self.b_imp], writes=[self.b_mx])
                wk = self.impw[:, 0, :]
                S.op("dve", lambda e: e.match_replace(out=wk, in_to_replace=self.mx[:, 0:8], in_values=ig, imm_value=-3.0e38),
                     reads=[self.b_imp, self.b_mx], writes=[self.b_impw])
                S.op("dve", lambda e: e.max(out=self.mx[:, 8:16], in_=wk), reads=[self.b_impw], writes=[self.b_mx])
                S.op("dve", lambda e: e.tensor_scalar(out=self.mbq[:, gg, :], in0=ig, scalar1=self.mx[:, 15:16], scalar2=None,
                                                      op0=ALU.is_ge), reads=[self.b_imp, self.b_mx], writes=[self.b_mbq])
                S.op("pe", lambda e: e.transpose(out=self.pT[0:64, gg * 128:(gg + 1) * 128], in_=self.mbq[:, gg, :], identity=self.ident[:]),
                     reads=[self.b_mbq, self.b_ident], writes=[self.b_pT], sig=True)
                S.op("act", lambda e: e.activation(out=MBT[:, gg, :], in_=self.pT[0:64, gg * 128:(gg + 1) * 128], func=AF.Copy),
                     reads=[self.b_pT], writes=[self.b_mbt] + ring_guard)

            acc_banks = [(self.pY[:, 0:260], self.b_pY), (self.pY[:, 512:772], self.b_pY), (self.pA[2][:, 0:260], self.b_pA[2]),
                         (self.pA[3][:, 0:260], self.b_pA[3])]

            def post_br(br, gg):
                accb, bacc = acc_banks[gg]
                av = accb.rearrange("p (r c) -> p r c", c=65)
                rd = gs[:, 8:12].rearrange("p (r o) -> p r o", o=1)
                gcol = self.gates[:, t, (br + 1) * 16 + gg * 4:(br + 1) * 16 + gg * 4 + 4].rearrange("p (r o) -> p r o", o=1)
                S.op("dve", lambda e: e.tensor_scalar(out=rd, in0=av[:, :, 64:65], scalar1=1e-30, scalar2=None, op0=ALU.max),
                     reads=[bacc], writes=[self.b_gsm])
                S.op("dve", lambda e: e.reciprocal(out=rd, in_=rd), reads=[self.b_gsm], writes=[self.b_gsm])
                S.op("dve", lambda e: e.tensor_tensor(out=rd, in0=rd, in1=gcol, op=ALU.mult), reads=[self.b_gsm, self.b_gates], writes=[self.b_gsm])
                ow = self.impw[:, :, :]
                S.op("dve", lambda e: e.tensor_tensor(out=ow, in0=av[:, :, 0:64], in1=rd.to_broadcast([128, 4, 64]), op=ALU.mult),
                     reads=[bacc, self.b_gsm], writes=[self.b_impw])
                oc = o[:, gg * 256:(gg + 1) * 256].rearrange("p (r c) -> p r c", c=64)
                S.op("pool", lambda e: e.tensor_tensor(out=oc, in0=oc, in1=ow, op=ALU.add), reads=[self.b_impw] + bo, writes=bo)

            steps = []
            for gg in range(4):
                accs = [self.pY[:, (r // 2) * 512 + (r % 2) * 129: (r // 2) * 512 + (r % 2) * 129 + 129] for r in range(4)]
                for ii, nt in enumerate(nts):
                    steps.append(dict(gg=gg, tile=None, selkt=None,
                                      kv=(lambda gg=gg, nt=nt: (self.kcT[0:71, gg, nt * 128:(nt + 1) * 128], [self.b_kcT, self.b_c2],
                                                               self.vc[:, nt, gg, :], [self.b_vc, self.b_c2])),
                                      mask=(lambda gg=gg, nt=nt: (self.ident[:], bc4(self.msk[:, mi, nt, :]), [self.b_ident, self.b_msk[mi]])),
                                      accs=accs, bacc=self.b_pY, first=(ii == 0), last=(ii == len(nts) - 1), start_r=(0, 2),
                                      post=((lambda gg=gg: post_cmp(gg)) if ii == len(nts) - 1 else None)))
            tiles = []
            for br in range(2):
                kts = list(range(0, T + 1)) if br == 0 else list(range(max(0, T - 4), T + 1))
                for ii, kt in enumerate(kts):
                    tidx = len(tiles)
                    tiles.append((br, kt))
                    for gg in range(4):
                        accb, bacc = acc_banks[gg]
                        accs = [accb[:, r * 65:(r + 1) * 65] for r in range(4)]
                        if kt == T:
                            mk = (lambda gg=gg: (self.ident[:], bc4(self.mbd[:, 0, :]), [self.b_ident, self.b_c2]))
                        elif br == 0:
                            mk = None
                        elif kt == T - 4:
                            mk = (lambda gg=gg: (self.ident[:], bc4(self.mbd[:, 1, :]), [self.b_ident, self.b_c2]))
                        else:
                            mk = None
                        last = (ii == len(kts) - 1)
                        steps.append(dict(gg=gg, tile=tidx, kv=None, mask=mk, accs=accs, bacc=bacc, first=(ii == 0), last=last,
                                          start_r=(0,), post=((lambda br=br, gg=gg: post_br(br, gg)) if last else None),
                                          selkt=(kt if (br == 0 and kt != T) else None)))
            loaded = {}
            mexp = {}
            pT32 = self.pT[:, :].bitcast(F32)
            sbanks = [(self.pA[0], self.b_pA[0]), (self.pA[1], self.b_pA[1]), (self.pM, self.b_pM)]
            if not hasattr(self, "sp_i"):
                self.sp_i = 0

            def ensure_load(idx):
                if idx < len(tiles) and idx not in loaded:
                    loaded[idx] = self.kv_tile_load(tiles[idx][0], tiles[idx][1], kview, vview)

            def stage_a(k, st_):
                gg = st_["gg"]
                if st_["tile"] is not None:
                    ensure_load(st_["tile"])
                    ensure_load(st_["tile"] + 1)
                    kdst, vdst, bk, bv = loaded[st_["tile"]]
                    kT_ap, bkl, v_ap, bvl = kdst[:, gg, :], [bk] + ring_guard, vdst[:, gg * 65:(gg + 1) * 65], [bv]
                else:
                    kT_ap, bkl, v_ap, bvl = st_["kv"]()
                st_["v"] = (v_ap, bvl)
                if st_["selkt"] is not None:
                    if gg == 0:
                        for kt_ in (st_["selkt"], st_["selkt"] + 1):
                            if kt_ >= T or kt_ in mexp:
                                continue
                            S.op("pe", lambda e: e.matmul(pT32, lhsT=fm[:, kt_ * 128:(kt_ + 1) * 128], rhs=MBT[:, :, :], start=True, stop=True),
                                 reads=b_fm + [self.b_mbt], writes=[self.b_pT], sig=True)
                            mi_ = self.m01_i % 3
                            self.m01_i += 1
                            S.op("dve", lambda e: e.tensor_copy(out=m01v[:, mi_, :], in_=pT32),
                                 reads=[self.b_pT], writes=[self.b_m01[mi_]] + ring_guard)
                            mexp[kt_] = mi_
                    st_["m01"] = mexp[st_["selkt"]]
                sp, bsp = sbanks[self.sp_i % 3]
                self.sp_i += 1
                st_["sp"] = (sp, bsp)
                mm = st_["mask"]() if st_["mask"] is not None else None
                rhs_q = QT[0:71, gg * 4:(gg + 1) * 4, qs]
                S.op("pe", lambda e: e.matmul(sp[:], lhsT=kT_ap, rhs=rhs_q, start=True, stop=(mm is None)),
                     reads=bkl + b_QT, writes=[bsp], sig=(mm is None))
                if mm is not None:
                    mlhs, mrhs, mb_ = mm
                    S.op("pe", lambda e: e.matmul(sp[:], lhsT=mlhs, rhs=mrhs, start=False, stop=True), reads=mb_, writes=[bsp], sig=True)

            def stage_bc(st_):
                sp, bsp = st_["sp"]
                v_ap, bvl = st_["v"]
                pi_ = self.pt_i % 3
                self.pt_i += 1
                pt = pview[:, pi_]
                S.op("act", lambda e: e.activation(out=pt, in_=sp[:], func=AF.Exp), reads=[bsp], writes=[self.b_pt[pi_]])
                if st_["selkt"] is not None:
                    mi_ = st_["m01"]
                    gg_ = st_["gg"]
                    pt4 = pt.rearrange("p (r q) -> p r q", r=4)
                    S.op("dve", lambda e: e.tensor_tensor(out=pt4, in0=pt4, in1=bc4(m01v[:, mi_, gg_ * 128:(gg_ + 1) * 128]), op=ALU.mult),
                         reads=[self.b_pt[pi_], self.b_m01[mi_]], writes=[self.b_pt[pi_]])
                for r in range(4):
                    S.op("pe", lambda e: e.matmul(st_["accs"][r], lhsT=pt[:, r * 128:(r + 1) * 128], rhs=v_ap,
                                                  start=(st_["first"] and r in st_["start_r"]), stop=st_["last"]),
                         reads=[self.b_pt[pi_]] + bvl, writes=[st_["bacc"]], sig=(st_["last"] and r == 3))
                if st_["post"] is not None:
                    st_["post"]()

            ncmp = 4 * len(nts)
            for lst in (steps[:ncmp], steps[ncmp:]):
                for k0 in range(min(2, len(lst))):
                    stage_a(k0, lst[k0])
                for k in range(len(lst)):
                    if k + 2 < len(lst):
                        stage_a(k + 2, lst[k + 2])
                    stage_bc(lst[k])
            if getattr(self, "dbg_o", False):
                if "dbg" not in S.semh:
                    S.newsem("dbg")
                S.dma("sp", self.out[g * TG + t * 128: g * TG + (t + 1) * 128, :], o[:], "dbg", reads=bo, writes=[self.b_out])
            xi = self.xn_i % 2
            self.xn_i += 1
            xn = self.xn[:, xi, :]
            S.op("act", lambda e: e.activation(out=xn, in_=o[:], func=AF.Copy), reads=bo, writes=[self.b_xn[xi]])
            for c in range(8):
                S.op("pe", lambda e: e.transpose(out=self.pT[:, c * 128:(c + 1) * 128], in_=xn[:, c * 128:(c + 1) * 128], identity=self.ident[:]),
                     reads=[self.b_xn[xi], self.b_ident], writes=[self.b_pT], sig=(c == 7))
            S.op("dve", lambda e: e.tensor_copy(out=self.hT[:, :, qs], in_=self.pT[:, :].rearrange("p (c q) -> p c q", c=8)),
                 reads=[self.b_pT], writes=[self.b_hT[t]])
        for t in range(NT):
            yb = self.ybank(t)
            for dh in range(2):
                for kc in range(8):
                    S.op("pe", lambda e: e.matmul(yb[dh], lhsT=self.hT[:, kc, t * 128:(t + 1) * 128],
                                                  rhs=self.wbig[:, 8576 + kc * D + dh * 512: 8576 + kc * D + (dh + 1) * 512],
                                                  start=(kc == 0), stop=(kc == 7)),
                         reads=[self.b_hT[t], self.b_wbig], writes=yb[2], sig=(kc == 7))
            self.post_tile(t, cbv, bcb)
        S.wait_all("pool", self.b_kt + self.b_vt + self.b_pt + [self.b_mbt] + self.b_m01)

_PROG_CACHE = {}


def _device_inputs(inp, ntok=SEQ):
    shared = host_layout(inp)
    shared.update(host_constants())
    maps = []
    for b in range(8):
        f = dict(shared)
        f.update(host_percore(inp, b, ntok))
        maps.append(f)
    return maps


def kernel(**inputs):
    inp = {k: np.asarray(v) for k, v in inputs.items()}
    in_maps = _device_inputs(inp)
    shapes = {k: v.shape for k, v in in_maps[0].items()}
    prog = Prog()
    nc = prog.build(shapes)
    res = run_bass_kernel_spmd(nc, in_maps, core_ids=list(range(8)))
    out = np.stack([np.asarray(r["out"], dtype=np.float32) for r in res.results], axis=0)
    return out
```

```python
import contextlib
import numpy as np
import concourse.bass as bass
import concourse.mybir as mybir
from concourse.bass_utils import run_bass_kernel_spmd

F32 = mybir.dt.float32
BF16 = mybir.dt.bfloat16
AF = mybir.ActivationFunctionType
ALU = mybir.AluOpType
AX = mybir.AxisListType

D = 1024
SEQ = 4096
DFF = 2816
NJ = DFF // 128
NT = 8
TG = NT * 128
NGRP = SEQ // TG
NNG = TG // 512
EPS = 1e-6
NL = 4


class Buf:
    __slots__ = ("name", "w", "r")

    def __init__(self, name):
        self.name = name
        self.w = None
        self.r = []


class Sched:
    def __init__(self, nc, ctx):
        self.nc = nc
        self.ctx = ctx
        self.E = {"pe": nc.tensor, "act": nc.scalar, "dve": nc.vector, "pool": nc.gpsimd, "sp": nc.sync}
        self.semh = {}
        self.cnt = {}
        self.seen = {e: {} for e in self.E}
        self.n_inst = 0
        self.n_wait = 0
        for e in ("pe", "act", "dve", "pool"):
            self.newsem(e)

    def newsem(self, key):
        self.semh[key] = self.ctx.enter_context(self.nc.semaphore("s_" + key))
        self.cnt[key] = 0
        return key

    def _wait(self, eng, tok):
        if tok is None:
            return
        key, val = tok
        if self.seen[eng].get(key, 0) >= val:
            return
        assert val <= self.cnt[key], f"wait on unsignaled token {tok} cnt={self.cnt[key]} from {eng}"
        self.E[eng].wait_ge(self.semh[key], val)
        self.seen[eng][key] = val
        self.n_wait += 1

    def _deps(self, eng, reads, writes):
        for b in reads:
            self._wait(eng, b.w)
        for b in writes:
            if b.w is not None and b.w[0] != eng:
                self._wait(eng, b.w)
            for t in b.r:
                if t[0] != eng:
                    self._wait(eng, t)

    def _commit(self, tok, reads, writes):
        for b in reads:
            b.r.append(tok)
            if len(b.r) > 48:
                best = {}
                for k, v in b.r:
                    if best.get(k, 0) < v:
                        best[k] = v
                b.r = list(best.items())
        for b in writes:
            b.w = tok
            b.r = []

    def op(self, eng, fn, reads=(), writes=(), sig=True):
        self._deps(eng, reads, writes)
        ins = fn(self.E[eng])
        self.n_inst += 1
        if sig:
            ins.then_inc(self.semh[eng], 1)
            self.cnt[eng] += 1
            tok = (eng, self.cnt[eng])
        else:
            tok = (eng, self.cnt[eng] + 1)
        self._commit(tok, reads, writes)
        return tok

    def dma(self, q, out, in_, semkey, reads=(), writes=(), **kw):
        self._deps(q, reads, writes)
        ins = self.E[q].dma_start(out=out, in_=in_, **kw)
        ins.then_inc(self.semh[semkey], 16)
        self.n_inst += 1
        self.cnt[semkey] += 16
        tok = (semkey, self.cnt[semkey])
        self._commit(tok, reads, writes)
        return tok

    def wait_all(self, eng, bufs):
        for b in bufs:
            self._wait(eng, b.w)
            for t in b.r:
                self._wait(eng, t)


def _ktile(w):
    K, N = w.shape
    return np.ascontiguousarray(w.reshape(K // 128, 128, N).transpose(1, 0, 2).reshape(128, (K // 128) * N))


def _colT(v, nchunk):
    return np.ascontiguousarray(v.reshape(nchunk, 128).T)


def _bf16_split(a, n):
    import ml_dtypes
    out = []
    r = np.asarray(a, np.float64)
    for _ in range(n):
        p = r.astype(np.float32).astype(ml_dtypes.bfloat16).astype(np.float32)
        out.append(p)
        r = r - p.astype(np.float64)
    return out


def host_constants():
    c = {}
    c["ident"] = np.eye(128, dtype=np.float32)
    t = np.arange(SEQ)
    p = (t % 128).astype(np.float32)
    kt = (128 * (t // 128)).astype(np.float32)
    one = np.ones(SEQ, np.float32)
    ka = np.stack([p, p, kt, kt, one, one, one])
    ka = ka.reshape(7, SEQ // 128, 128).transpose(1, 0, 2)
    c["kaug"] = np.ascontiguousarray(np.broadcast_to(ka[:, :, None, :], (SEQ // 128, 7, 4, 128)))
    n = np.arange(256)
    a = (31 + 16 * (n % 8)).astype(np.float32)
    bb = (128 * (n // 8)).astype(np.float32)
    o2 = np.ones(256, np.float32)
    kc = np.stack([a, a, bb, bb, o2, o2, o2])
    c["kaugc"] = np.ascontiguousarray(np.broadcast_to(kc[:, None, :], (7, 4, 256)))
    h = np.arange(1, 17, dtype=np.float64)
    slope = np.exp2(-8.0 * h / 16)
    s_hi, s_lo = _bf16_split(slope, 2)
    s_eff = s_hi.astype(np.float64) + s_lo.astype(np.float64)
    cq = -(s_eff[:, None] * t[None, :].astype(np.float64))
    c1, c2, c3 = _bf16_split(cq, 3)
    bc = lambda v: np.broadcast_to(v[:, None], (16, SEQ)).astype(np.float32)
    c["qaug"] = np.ascontiguousarray(np.stack([bc(s_hi), bc(s_lo), bc(s_hi), bc(s_lo), c1, c2, c3]))
    j = np.arange(64)
    c["fmat"] = np.where((t[None, :] // 64) == j[:, None], 1.0, 0.0).astype(np.float32)
    k = np.arange(128)
    c["mbdiag"] = np.where(k[:, None] <= k[None, :], 0.0, -30000.0).astype(np.float32)
    c["mbfar"] = np.where(k[:, None] > k[None, :], 0.0, -30000.0).astype(np.float32)
    c["tcmp"] = np.where(16 * k[:, None] + 31 <= t[None, :], 0.0, -30000.0).astype(np.float32)
    tq = t.reshape(32, 128)
    cur = tq // 64
    jj = j[None, None, :]
    fb = np.zeros((32, 128, 64), np.float32)
    fb = np.where(64 * jj > tq[:, :, None], -1e30, fb)
    fb = np.where(jj == cur[:, :, None] - 1, 1e30, fb)
    fb = np.where(jj == cur[:, :, None], 2e30, fb)
    fb = np.where(jj == 0, 3e30, fb)
    c["fbias"] = np.ascontiguousarray(fb.astype(np.float32))
    nn = np.arange(256)
    ov = ((16 * nn[:, None] < 64 * j[None, :] + 64) & (16 * nn[:, None] + 31 >= 64 * j[None, :])).astype(np.float32)
    c["ovl"] = np.ascontiguousarray(ov.reshape(2, 128, 64).transpose(1, 0, 2))
    return c


def host_percore(inp, b, ntok=SEQ):
    f = {}
    f["x"] = np.ascontiguousarray(inp["x"][b][:ntok])
    f["ccol"] = _colT(np.asarray(inp["c"][b]), 8)
    return f


def host_layout(inp):
    f = {}
    ada_w = inp["ada_w"]
    f["adaw"] = np.ascontiguousarray(
        np.stack([np.stack([_ktile(ada_w[l][:, u * 256:(u + 1) * 256]) for u in range(36)]) for l in range(NL)]))
    ada_b = inp["ada_b"]
    f["adabT"] = np.ascontiguousarray(np.concatenate([_colT(ada_b[l], 72) for l in range(NL)], axis=1))
    g = np.stack([np.stack([ada_b[l].reshape(3, 3, D)[s, 2] for s in range(3)]) for l in range(NL)])
    f["adabg"] = np.ascontiguousarray(np.broadcast_to(g[:, :, None, :], (NL, 3, 128, D)))
    ng = inp["norm_g"]
    f["gpost"] = np.ascontiguousarray(np.broadcast_to(ng[:, :, 1][:, :, None, :], (NL, 3, 128, D)))
    f["gpreT"] = np.ascontiguousarray(
        np.concatenate([_colT(ng[l, s, 0], 8) for l in range(NL) for s in range(3)], axis=1))
    f["kvadaw"] = np.ascontiguousarray(np.stack([_ktile(inp["kv_ada_w"][:, u * 256:(u + 1) * 256]) for u in range(8)]))
    f["kvadabT"] = _colT(inp["kv_ada_b"], 16)
    f["kvgT"] = _colT(inp["kv_norm_g"], 8)
    wi = inp["ffn_w_in"]
    f["ffnin"] = np.ascontiguousarray(np.stack([np.stack([np.stack([
        _ktile(np.concatenate([wi[l, s][:, j * 128:(j + 1) * 128], wi[l, s][:, DFF + j * 128:DFF + (j + 1) * 128]], axis=1))
        for j in range(NJ)]) for s in range(2)]) for l in range(NL)]))
    wo = inp["ffn_w_out"]
    f["ffnout"] = np.ascontiguousarray(np.stack([np.stack([_ktile(wo[l, s]) for s in range(2)]) for l in range(NL)]))
    aw = inp["a_w_in"]
    f["awin"] = np.ascontiguousarray(np.stack([np.stack([
        _ktile(np.concatenate([aw[l][:, k * D + c * 128:k * D + (c + 1) * 128] for k in range(3)], axis=1))
        for c in range(8)]) for l in range(2)]))
    f["awout"] = np.ascontiguousarray(np.stack([_ktile(inp["a_w_out"][l]) for l in range(2)]))
    f["kvw"] = _ktile(inp["kv_w"])
    f["bwin"] = np.ascontiguousarray(np.stack([_ktile(inp["b_w_in"][l]) for l in range(2)]))
    f["bwout"] = np.ascontiguousarray(np.stack([_ktile(inp["b_w_out"][l]) for l in range(2)]))
    f["cmpw1"] = np.ascontiguousarray(inp["cmp_w1"].reshape(2, 32, 64, 256).transpose(2, 0, 1, 3).reshape(64, 16384))
    f["cmpw2"] = np.ascontiguousarray(inp["cmp_w2"].reshape(2, 2, 128, 64).transpose(2, 0, 1, 3).reshape(128, 256))
    f["cmpposT"] = np.ascontiguousarray(inp["cmp_pos"].transpose(2, 0, 1).reshape(64, 64))
    ac = inp["a_conv"]
    f["aconvT"] = np.ascontiguousarray(np.concatenate(
        [np.stack([_colT(ac[l, k], 8) for k in range(3)], axis=2).reshape(128, 24) for l in range(2)], axis=1))
    return f


class Prog:
    def __init__(self, ngrp=NGRP, layers=NL, sub_limit=None):
        self.ngrp = ngrp
        self.layers = layers
        self.sub_limit = sub_limit
        self.nc = bass.Bass("TRN2", target_bir_lowering=False)
        self.din = {}

    def dram_in(self, name, shape):
        t = self.nc.dram_tensor(name, list(shape), F32, kind="ExternalInput")
        self.din[name] = t
        return t.ap()

    def sb(self, name, shape, dt):
        return self.ctx.enter_context(self.nc.sbuf_tensor("sb_" + name, list(shape), dt))

    def ps(self, name, shape, dt):
        return self.ctx.enter_context(self.nc.psum_tensor("ps_" + name, list(shape), dt))

    def build(self, shapes):
        nc = self.nc
        A = {k: self.dram_in(k, v) for k, v in shapes.items()}
        self.A = A
        self.out = nc.dram_tensor("out", [self.ngrp * TG, D], F32, kind="ExternalOutput").ap()
        self.cbscr = nc.dram_tensor("cbscr", [NL * 3, 128, D], F32, kind="Internal").ap()
        self.kscr = nc.dram_tensor("kscr", [2, SEQ // 128, 71, 4, 128], BF16, kind="Internal").ap()
        self.vscr = nc.dram_tensor("vscr", [2, SEQ // 128, 128, 260], BF16, kind="Internal").ap()
        with contextlib.ExitStack() as ctx:
            self.ctx = ctx
            S = self.S = Sched(nc, ctx)
            sb, ps = self.sb, self.ps
            self.xres = sb("xres", [128, NT, D], F32)
            self.hT = sb("hT", [128, 8, TG], BF16)
            self.actT = sb("actT", [128, NJ, TG], BF16)
            self.wbig = sb("wbig", [128, NJ * D], BF16)
            self.ring = sb("ring", [128, 3, 3072], BF16)
            self.cb = sb("cb", [128, 2, D], F32)
            self.xn = sb("xn", [128, 2, D], BF16)
            self.tmpb = sb("tmpb", [128, 2, 512], BF16)
            self.tmp32 = sb("tmp32", [128, D], F32)
            self.junk = sb("junk", [128, D], BF16)
            self.ident = sb("ident", [128, 128], BF16)
            self.st = sb("stat", [128, 64], F32)
            self.modT = sb("modT", [128, 13 * 16], F32)
            self.sc = sb("sc", [128, 8], F32)
            self.scb = sb("scb", [128, 8], BF16)
            self.screp = sb("screp", [128, 8, 128], BF16)
            self.pro = sb("pro", [128, 420], F32)
            self.halo = sb("halo", [128, 32], F32)
            self.convw = sb("convw", [128, 48], F32)
            self.nhalf = sb("nhalf", [128, 8], F32)
            self.kcT = sb("kcT", [128, 4, 256], BF16)
            self.vc = sb("vc", [128, 2, 4, 129], BF16)
            self.hidV = sb("hidV", [128, 2, 4, 256], BF16)
            self.cmphalo = sb("cmphalo", [64, 2, 4, 16], BF16)
            self.biash = sb("biash", [128, 4], F32)
            self.w2 = sb("w2", [128, 256], BF16)
            self.posT = sb("posT", [64, 64], BF16)
            self.vst = sb("vst", [128, 2, 260], BF16)
            self.gates = sb("gates", [128, NT, 48], F32)
            self.imp = sb("imp", [128, 4, 64], F32)
            self.impw = sb("impw", [128, 4, 64], F32)
            self.mx = sb("mx", [128, 16], F32)
            self.mbq = sb("mbq", [128, 4, 64], BF16)
            self.msk = sb("msk", [128, 2, 2, 128], BF16)
            self.fbs = sb("fbs", [128, 2, 64], F32)
            self.mbd = sb("mbd", [128, 2, 128], BF16)
            self.gsm = sb("gsm", [128, 64], F32)
            self.cmpt = sb("cmpt", [128, 8, 64], F32)
            self.hidK = sb("hidK", [128, 2, 64], BF16)
            self.pT = ps("pT", [128, D], BF16)
            self.pA = [ps(f"pA{i}", [128, 512], F32) for i in range(4)]
            self.pY = ps("pY", [128, D], F32)
            self.pM = ps("pM", [128, 512], F32)
            B = lambda n: Buf(n)
            self.b_xres = [B(f"xres{t}") for t in range(NT)]
            self.b_hT = [B(f"hT{t}") for t in range(NT)]
            self.b_actT = [[B(f"actT{j}_{n}") for n in range(NNG)] for j in range(NJ)]
            self.b_wbig = B("wbig")
            self.b_ring = [B(f"ring{i}") for i in range(3)]
            self.b_cb = [B("cb0"), B("cb1")]
            self.b_xn = [B("xn0"), B("xn1")]
            self.b_tmpb = [B("tmpb0"), B("tmpb1")]
            self.b_tmp32 = B("tmp32")
            self.b_junk = B("junk")
            self.b_const = B("const")
            self.b_ident = B("ident")
            self.b_st = B("st")
            self.b_modT = B("modT")
            self.b_pT = B("pT")
            self.b_pA = [B(f"pA{i}") for i in range(4)]
            self.b_pY = B("pY")
            self.b_pM = B("pM")
            self.b_cbscr = [B(f"cbscr{i}") for i in range(NL * 3)]
            self.b_out = B("out")
            self.b_halo = B("halo")
            self.b_kscr = [B("kscr0"), B("kscr1")]
            self.b_vscr = [B("vscr0"), B("vscr1")]
            self.b_kcT = B("kcT"); self.b_vc = B("vc"); self.b_hidV = B("hidV"); self.b_cmph = B("cmph")
            self.b_biash = B("biash"); self.b_c2 = B("c2"); self.b_vst = [B("vst0"), B("vst1")]
            self.b_gates = B("gates"); self.b_imp = B("imp"); self.b_impw = B("impw"); self.b_mx = B("mx")
            self.b_mbq = B("mbq"); self.b_msk = [B("msk0"), B("msk1")]; self.b_gsm = B("gsm")
            self.b_cmpt = B("cmpt"); self.b_hidK = B("hidK")
            for k in ("w1", "c2", "vst0", "vst1", "kst", "msk0", "msk1", "qaug", "fmat", "kt0", "kt1", "kt2", "vt0", "vt1", "vt2"):
                S.newsem(k)
            self.vst_i = 0
            self.msk_i = 0
            self.kv_i = 0
            for i in range(3):
                S.newsem(f"ring{i}")
            for k in ("wbig", "cb0", "cb1", "xin", "xout", "misc", "cbst0", "cbst1", "identl"):
                S.newsem(k)
            self.ring_i = 0
            self.cb_i = 0
            self.xn_i = 0

            self.emit()
            S.wait_all("sp", [self.b_out])
            self.stats = (S.n_inst, S.n_wait)
        return nc

    def ring_load(self, src_ap, ncols):
        i = self.ring_i % 3
        self.ring_i += 1
        dst = self.ring[:, i, 0:ncols]
        self.S.dma("pool", dst, src_ap, f"ring{i}", writes=[self.b_ring[i]])
        return i

    def emit(self):
        S = self.S
        A = self.A
        S.dma("pool", self.ident[:], A["ident"], "identl", writes=[self.b_ident])
        self.prologue()
        subs = []
        for l in range(self.layers):
            subs += [("ffn", l, 0), ("mix", l, 1), ("ffn", l, 2)]
        if self.sub_limit is not None:
            subs = subs[: self.sub_limit]
        for g in range(self.ngrp):
            S.dma("sp", self.xres[:], A["x"][g * TG:(g + 1) * TG, :].rearrange("(t p) d -> p t d", p=128), "xin",
                  writes=self.b_xres)
            for (kind, l, s) in subs:
                if kind == "ffn":
                    self.ffn(l, s)
                elif l < 2:
                    self.mixer_a(l, g)
                else:
                    self.nsa(l, g)
                if kind == "ffn" and l == 1 and s == 2:
                    self.kv_phase(g)
            if not getattr(self, "dbg_o", False):
                S.dma("sp", self.out[g * TG:(g + 1) * TG, :].rearrange("(t p) d -> p t d", p=128), self.xres[:], "xout",
                      reads=self.b_xres, writes=[self.b_out])

    def prologue(self):
        S, A = self.S, self.A
        pro = self.pro
        S.dma("sp", pro[:, 0:288], A["adabT"], "misc", writes=[self.b_const])
        S.dma("sp", pro[:, 288:384], A["gpreT"], "misc", writes=[self.b_const])
        S.dma("sp", pro[:, 384:392], A["ccol"], "misc", writes=[self.b_const])
        S.dma("sp", self.convw[:], A["aconvT"], "misc", writes=[self.b_const])
        S.op("dve", lambda e: e.memset(self.halo[:], 0.0), writes=[self.b_halo])
        S.op("dve", lambda e: e.memset(self.nhalf[:], -0.5), writes=[self.b_halo])
        S.op("dve", lambda e: e.memset(self.nhalf[:, 1:2], EPS), writes=[self.b_halo])
        S.op("act", lambda e: e.activation(out=self.sc[:], in_=pro[:, 384:392], func=AF.Silu),
             reads=[self.b_const], writes=[self.b_st])
        S.op("dve", lambda e: e.tensor_copy(out=self.scb[:], in_=self.sc[:]), reads=[self.b_st], writes=[self.b_modT])
        for kc in range(8):
            S.op("dve", lambda e: e.tensor_copy(out=self.screp[:, kc, :], in_=self.sc[:, kc:kc + 1].to_broadcast([128, 128])),
                 reads=[self.b_st], writes=[self.b_modT])
        if self.layers > 2 or (self.layers == 2 and (self.sub_limit is None or self.sub_limit >= 6)):
            self.nsa_init()
        for l in range(self.layers):
            for u in range(36):
                sub, kind, q = u // 12, (u % 12) // 4, u % 4
                ri = self.ring_load(A["adaw"][l, u], 2048)
                rb = self.b_ring[ri]
                un = self.ring[:, ri, :]
                si = l * 3 + sub
                if kind < 2:
                    self.mod_cols(un, rb, 2, pro[:, l * 72 + u * 2: l * 72 + u * 2 + 2],
                                  si * 16 + (8 if kind == 0 else 0) + q * 2, kind,
                                  pro[:, 288 + si * 8 + q * 2: 288 + si * 8 + q * 2 + 2])
                else:
                    for kc in range(8):
                        S.op("pe", lambda e: e.matmul(self.pA[0][:, 0:256], lhsT=self.screp[:, kc, :], rhs=un[:, kc * 256:(kc + 1) * 256],
                                                      start=(kc == 0), stop=(kc == 7)),
                             reads=[rb, self.b_modT], writes=[self.b_pA[0]], sig=(kc == 7))
                    ci = self.cb_i % 2
                    self.cb_i += 1
                    cbv = self.cb[:, ci, :]
                    bcb = self.b_cb[ci]
                    S.dma("sp", cbv[:, 0:256], A["adabg"][l, sub, :, q * 256:(q + 1) * 256], f"cb{ci}", writes=[bcb])
                    S.dma("sp", cbv[:, 256:512], A["gpost"][l, sub, :, q * 256:(q + 1) * 256], f"cb{ci}", writes=[bcb])
                    wgt = 1.0 if sub == 1 else 0.5
                    S.op("dve", lambda e: e.tensor_tensor(out=cbv[:, 0:256], in0=self.pA[0][:, 0:256], in1=cbv[:, 0:256], op=ALU.add),
                         reads=[self.b_pA[0], bcb], writes=[bcb])
                    S.op("dve", lambda e: e.scalar_tensor_tensor(out=cbv[:, 0:256], in0=cbv[:, 0:256], scalar=wgt, in1=cbv[:, 256:512],
                                                                 op0=ALU.mult, op1=ALU.mult),
                         reads=[bcb], writes=[bcb])
                    S.dma("sp", self.cbscr[si, :, q * 256:(q + 1) * 256], cbv[:, 0:256], f"cbst{ci}",
                          reads=[bcb], writes=[self.b_cbscr[si]])

    def nsa_init(self):
        S, A = self.S, self.A
        pro = self.pro
        S.dma("sp", pro[:, 392:408], A["kvadabT"], "misc", writes=[self.b_const])
        S.dma("sp", pro[:, 408:416], A["kvgT"], "misc", writes=[self.b_const])
        S.dma("pool", self.w2[:], A["cmpw2"], "c2", writes=[self.b_c2])
        S.dma("pool", self.posT[:], A["cmpposT"], "c2", writes=[self.b_c2])
        S.dma("pool", self.mbd[:, 0, :], A["mbdiag"], "c2", writes=[self.b_c2])
        S.dma("pool", self.mbd[:, 1, :], A["mbfar"], "c2", writes=[self.b_c2])
        S.dma("pool", self.kcT[64:71, :, :], A["kaugc"], "c2", writes=[self.b_c2])
        for nt in range(2):
            for gg in range(4):
                S.dma("pool", self.vc[:, nt, gg, 65:129], A["ovl"][:, nt, :], "c2", writes=[self.b_c2])
        for br in range(2):
            S.dma("pool", self.kscr[br, :, 64:71, :, :], A["kaug"], "c2", writes=[self.b_c2])
        S.op("dve", lambda e: e.memset(self.kcT[0:64, :, :], 0.0), reads=[self.b_c2], writes=[self.b_kcT])
        S.op("dve", lambda e: e.memset(self.vc[:, :, :, 0:64], 0.0), reads=[self.b_c2], writes=[self.b_vc])
        S.op("dve", lambda e: e.memset(self.vc[:, :, :, 64:65], 1.0), reads=[self.b_c2], writes=[self.b_vc])
        S.op("dve", lambda e: e.memset(self.hidV[:], 0.0), writes=[self.b_hidV])
        S.op("dve", lambda e: e.memset(self.cmphalo[:], 0.0), writes=[self.b_cmph])
        S.op("dve", lambda e: e.memset(self.vst[:], 1.0), writes=self.b_vst)
        for u in range(8):
            ri = self.ring_load(A["kvadaw"][u], 2048)
            kind, q = (0 if u < 4 else 1), u % 4
            self.mod_cols(self.ring[:, ri, :], self.b_ring[ri], 2, pro[:, 392 + u * 2: 392 + u * 2 + 2],
                          12 * 16 + (8 if kind == 0 else 0) + q * 2, kind, pro[:, 408 + q * 2: 408 + q * 2 + 2])

    def mod_cols(self, un, rb, ncc, bias_ap, dst0, kind, g_ap):
        S = self.S
        for cc in range(ncc):
            for kc in range(8):
                S.op("pe", lambda e: e.matmul(self.pM[:, cc:cc + 1], lhsT=un[:, kc * 256 + cc * 128: kc * 256 + (cc + 1) * 128],
                                              rhs=self.scb[:, kc:kc + 1], start=(kc == 0), stop=(kc == 7)),
                     reads=[rb, self.b_modT], writes=[self.b_pM], sig=(kc == 7 and cc == ncc - 1))
        dst = self.modT[:, dst0:dst0 + ncc]
        if kind == 0:
            S.op("dve", lambda e: e.tensor_tensor(out=dst, in0=self.pM[:, 0:ncc], in1=bias_ap, op=ALU.add),
                 reads=[self.b_pM, self.b_const], writes=[self.b_modT])
        else:
            S.op("dve", lambda e: e.scalar_tensor_tensor(out=dst, in0=self.pM[:, 0:ncc], scalar=1.0, in1=bias_ap, op0=ALU.add, op1=ALU.add),
                 reads=[self.b_pM, self.b_const], writes=[self.b_modT])
            S.op("dve", lambda e: e.tensor_tensor(out=dst, in0=dst, in1=g_ap, op=ALU.mult),
                 reads=[self.b_modT, self.b_const], writes=[self.b_modT])

    def norm_tile(self, t, si):
        S = self.S
        x = self.xres[:, t, :]
        st = self.st
        S.op("act", lambda e: e.activation(out=self.junk[:], in_=x, func=AF.Square, accum_out=st[:, t:t + 1]),
             reads=[self.b_xres[t]], writes=[self.b_junk, self.b_st])
        S.op("act", lambda e: e.activation(out=st[:, 8 + t:9 + t], in_=st[:, t:t + 1], func=AF.Sqrt, scale=1.0 / D, bias=self.nhalf[:, 1:2]),
             reads=[self.b_st, self.b_halo], writes=[self.b_st])
        S.op("dve", lambda e: e.reciprocal(out=st[:, 16 + t:17 + t], in_=st[:, 8 + t:9 + t]), reads=[self.b_st], writes=[self.b_st])
        xi = self.xn_i % 2
        self.xn_i += 1
        xn = self.xn[:, xi, :]
        S.op("act", lambda e: e.activation(out=xn, in_=x, func=AF.Copy, scale=st[:, 16 + t:17 + t]),
             reads=[self.b_xres[t], self.b_st], writes=[self.b_xn[xi]])
        for c in range(8):
            S.op("pe", lambda e: e.transpose(out=self.pT[:, c * 128:(c + 1) * 128], in_=xn[:, c * 128:(c + 1) * 128],
                                             identity=self.ident[:]),
                 reads=[self.b_xn[xi], self.b_ident], writes=[self.b_pT], sig=(c == 7))
        for c in range(8):
            a_col = self.modT[:, si * 16 + c: si * 16 + c + 1]
            s_col = self.modT[:, si * 16 + 8 + c: si * 16 + 8 + c + 1]
            dst = self.hT[:, c, t * 128:(t + 1) * 128]
            src = self.pT[:, c * 128:(c + 1) * 128]
            if t % 2 == 0:
                S.op("act", lambda e: e.activation(out=dst, in_=src, func=AF.Identity, bias=s_col, scale=a_col),
                     reads=[self.b_pT, self.b_modT], writes=[self.b_hT[t]])
            else:
                S.op("dve", lambda e: e.tensor_scalar(out=dst, in0=src, scalar1=a_col, scalar2=s_col, op0=ALU.mult, op1=ALU.add),
                     reads=[self.b_pT, self.b_modT], writes=[self.b_hT[t]])

    def ybank(self, t):
        if t % 2 == 0:
            return self.pY[:, 0:512], self.pY[:, 512:1024], [self.b_pY]
        return self.pA[0][:], self.pA[1][:], [self.b_pA[0], self.b_pA[1]]

    def post_tile(self, t, cbv, bcb):
        S = self.S
        st = self.st
        y0, y1, by = self.ybank(t)
        S.op("act", lambda e: e.activation(out=self.junk[:, 0:512], in_=y0, func=AF.Square, accum_out=st[:, 24 + t:25 + t]),
             reads=by, writes=[self.b_junk, self.b_st])
        S.op("act", lambda e: e.activation(out=self.junk[:, 512:1024], in_=y1, func=AF.Square, accum_out=st[:, 48 + t:49 + t]),
             reads=by, writes=[self.b_junk, self.b_st])
        S.op("dve", lambda e: e.tensor_tensor(out=st[:, 32 + t:33 + t], in0=st[:, 24 + t:25 + t], in1=st[:, 48 + t:49 + t], op=ALU.add),
             reads=[self.b_st], writes=[self.b_st])
        S.op("act", lambda e: e.activation(out=st[:, 56 + t:57 + t], in_=st[:, 32 + t:33 + t], func=AF.Sqrt, scale=1.0 / D, bias=self.nhalf[:, 1:2]),
             reads=[self.b_st, self.b_halo], writes=[self.b_st])
        S.op("dve", lambda e: e.reciprocal(out=st[:, 40 + t:41 + t], in_=st[:, 56 + t:57 + t]), reads=[self.b_st], writes=[self.b_st])
        S.op("dve", lambda e: e.scalar_tensor_tensor(out=self.tmp32[:, 0:512], in0=y0, scalar=st[:, 40 + t:41 + t], in1=cbv[:, 0:512],
                                                     op0=ALU.mult, op1=ALU.mult),
             reads=by + [self.b_st, bcb], writes=[self.b_tmp32])
        S.op("dve", lambda e: e.scalar_tensor_tensor(out=self.tmp32[:, 512:1024], in0=y1, scalar=st[:, 40 + t:41 + t], in1=cbv[:, 512:1024],
                                                     op0=ALU.mult, op1=ALU.mult),
             reads=by + [self.b_st, bcb], writes=[self.b_tmp32])
        S.op("dve", lambda e: e.tensor_tensor(out=self.xres[:, t, :], in0=self.xres[:, t, :], in1=self.tmp32[:], op=ALU.add),
             reads=[self.b_tmp32, self.b_xres[t]], writes=[self.b_xres[t]])

    def load_cb(self, si):
        S = self.S
        ci = self.cb_i % 2
        self.cb_i += 1
        S.dma("sp", self.cb[:, ci, :], self.cbscr[si], f"cb{ci}", reads=[self.b_cbscr[si]], writes=[self.b_cb[ci]])
        return self.cb[:, ci, :], self.b_cb[ci]

    def load_wbig(self, src, ncols, nsplit=4):
        S = self.S
        step = ncols // nsplit
        for i in range(nsplit):
            S.dma("pool", self.wbig[:, i * step:(i + 1) * step], src[:, i * step:(i + 1) * step], "wbig", writes=[self.b_wbig])

    def ffn(self, l, s):
        S, A = self.S, self.A
        si = l * 3 + s
        fs = 0 if s == 0 else 1
        cbv, bcb = self.load_cb(si)
        pre = [self.ring_load(A["ffnin"][l, fs, j], 2048) for j in range(2)]
        self.load_wbig(A["ffnout"][l, fs], NJ * D)
        for t in range(NT):
            self.norm_tile(t, si)
        slots = list(pre)
        for j in range(NJ):
            if j + 2 < NJ:
                slots.append(self.ring_load(A["ffnin"][l, fs, j + 2], 2048))
            ri = slots[j]
            un = self.ring[:, ri, :]
            rb = self.b_ring[ri]
            for n in range(NNG):
                pg, pu = (self.pA[0], self.pA[1]) if (j * NNG + n) % 2 == 0 else (self.pA[2], self.pA[3])
                bg, bu = (self.b_pA[0], self.b_pA[1]) if (j * NNG + n) % 2 == 0 else (self.b_pA[2], self.b_pA[3])
                hb = self.b_hT[n * 4:(n + 1) * 4]
                for kc in range(8):
                    S.op("pe", lambda e: e.matmul(pg[:], lhsT=un[:, kc * 256: kc * 256 + 128], rhs=self.hT[:, kc, n * 512:(n + 1) * 512],
                                                  start=(kc == 0), stop=(kc == 7)), reads=[rb] + hb, writes=[bg], sig=(kc == 7))
                for kc in range(8):
                    S.op("pe", lambda e: e.matmul(pu[:], lhsT=un[:, kc * 256 + 128: kc * 256 + 256], rhs=self.hT[:, kc, n * 512:(n + 1) * 512],
                                                  start=(kc == 0), stop=(kc == 7)), reads=[rb] + hb, writes=[bu], sig=(kc == 7))
                ti = (j * NNG + n) % 2
                tb = self.tmpb[:, ti, :]
                S.op("act", lambda e: e.activation(out=tb, in_=pg[:], func=AF.Silu), reads=[bg], writes=[self.b_tmpb[ti]])
                S.op("dve", lambda e: e.tensor_tensor(out=self.actT[:, j, n * 512:(n + 1) * 512], in0=tb, in1=pu[:], op=ALU.mult),
                     reads=[self.b_tmpb[ti], bu], writes=[self.b_actT[j][n]])
        for t in range(NT):
            n = t // 4
            yb = self.ybank(t)
            for dh in range(2):
                for j in range(NJ):
                    S.op("pe", lambda e: e.matmul(yb[dh], lhsT=self.actT[:, j, t * 128:(t + 1) * 128],
                                                  rhs=self.wbig[:, j * D + dh * 512: j * D + (dh + 1) * 512],
                                                  start=(j == 0), stop=(j == NJ - 1)),
                         reads=[self.b_actT[j][n], self.b_wbig], writes=yb[2], sig=(j == NJ - 1))
            self.post_tile(t, cbv, bcb)

    def mixer_a(self, l, g):
        S, A = self.S, self.A
        si = l * 3 + 1
        cbv, bcb = self.load_cb(si)
        pre = [self.ring_load(A["awin"][l, c], 3072) for c in range(2)]
        self.load_wbig(A["awout"][l], 8 * D)
        for t in range(NT):
            self.norm_tile(t, si)
        aT = self.actT
        fl = lambda j0, j1: aT[:, j0:j1, :].rearrange("p a b -> p (a b)")
        v32 = fl(8, 11).bitcast(F32)[:, 0:TG + 2]
        acc = fl(11, 13).bitcast(F32)[:, 0:TG]
        cgs = fl(13, 14).bitcast(F32)[:, 0:512]
        bs = fl(14, 15)[:, 0:TG]
        vb = lambda j0, j1: [self.b_actT[j][n] for j in range(j0, j1) for n in range(NNG)]
        b_v32, b_acc, b_cgs, b_bs = vb(8, 11), vb(11, 13), vb(13, 14), vb(14, 15)
        slots = list(pre)
        for c in range(8):
            if c + 2 < 8:
                slots.append(self.ring_load(A["awin"][l, c + 2], 3072))
            ri = slots[c]
            un = self.ring[:, ri, :]
            rb = self.b_ring[ri]
            for n in range(NNG):
                hb = self.b_hT[n * 4:(n + 1) * 4]
                for k in range(3):
                    for kc in range(8):
                        S.op("pe", lambda e: e.matmul(self.pA[k][:], lhsT=un[:, kc * 384 + k * 128: kc * 384 + (k + 1) * 128],
                                                      rhs=self.hT[:, kc, n * 512:(n + 1) * 512], start=(kc == 0), stop=(kc == 7)),
                             reads=[rb] + hb, writes=[self.b_pA[k]], sig=(kc == 7))
                S.op("act", lambda e: e.activation(out=cgs, in_=self.pA[1][:], func=AF.Copy), reads=[self.b_pA[1]], writes=b_cgs)
                S.op("act", lambda e: e.activation(out=bs[:, n * 512:(n + 1) * 512], in_=self.pA[0][:], func=AF.Copy),
                     reads=[self.b_pA[0]], writes=b_bs)
                S.op("dve", lambda e: e.tensor_tensor(out=v32[:, 2 + n * 512: 2 + (n + 1) * 512], in0=cgs, in1=self.pA[2][:], op=ALU.mult),
                     reads=b_cgs + [self.b_pA[2]], writes=b_v32)
            hc = (l * 8 + c) * 2
            wc = (l * 8 + c) * 3
            cw = lambda k: self.convw[:, wc + k: wc + k + 1]
            S.op("dve", lambda e: e.tensor_copy(out=v32[:, 0:2], in_=self.halo[:, hc:hc + 2]), reads=[self.b_halo], writes=b_v32)
            S.op("dve", lambda e: e.tensor_scalar(out=acc, in0=v32[:, 2:2 + TG], scalar1=cw(2), scalar2=None, op0=ALU.mult),
                 reads=b_v32 + [self.b_const], writes=b_acc)
            S.op("dve", lambda e: e.scalar_tensor_tensor(out=acc, in0=v32[:, 1:1 + TG], scalar=cw(1), in1=acc, op0=ALU.mult, op1=ALU.add),
                 reads=b_v32 + b_acc + [self.b_const], writes=b_acc)
            S.op("dve", lambda e: e.scalar_tensor_tensor(out=acc, in0=v32[:, 0:TG], scalar=cw(0), in1=acc, op0=ALU.mult, op1=ALU.add),
                 reads=b_v32 + b_acc + [self.b_const], writes=b_acc)
            S.op("dve", lambda e: e.tensor_copy(out=self.halo[:, hc:hc + 2], in_=v32[:, TG:TG + 2]), reads=b_v32, writes=[self.b_halo])
            S.op("dve", lambda e: e.tensor_tensor(out=aT[:, c, :], in0=bs, in1=acc, op=ALU.mult),
                 reads=b_bs + b_acc, writes=self.b_actT[c])
        for t in range(NT):
            n = t // 4
            yb = self.ybank(t)
            for dh in range(2):
                for kc in range(8):
                    S.op("pe", lambda e: e.matmul(yb[dh], lhsT=aT[:, kc, t * 128:(t + 1) * 128],
                                                  rhs=self.wbig[:, kc * D + dh * 512: kc * D + (dh + 1) * 512],
                                                  start=(kc == 0), stop=(kc == 7)),
                         reads=[self.b_actT[kc][n], self.b_wbig], writes=yb[2], sig=(kc == 7))
            self.post_tile(t, cbv, bcb)


    def kv_phase(self, g):
        S, A = self.S, self.A
        si = 12
        self.load_wbig(A["kvw"], 8 * 1536, nsplit=2)
        allact = lambda j0, j1: [self.b_actT[j][n] for j in range(j0, j1) for n in range(NNG)]
        w1v = self.actT[0:64, 0:16, :].rearrange("p a b -> p (a b)")
        b_w1 = allact(0, 16)
        for i in range(2):
            S.dma("pool", w1v[:, i * 8192:(i + 1) * 8192], A["cmpw1"][:, i * 8192:(i + 1) * 8192], "w1", writes=b_w1)
        kst = self.actT[0:64, 16:20, :]
        b_kst = allact(16, 20)
        for t in range(NT):
            self.norm_tile(t, si)
        stg = self.ring[0:64, :, :].rearrange("p a b -> p (a b)")[:, 0:8 * (TG + 16)].rearrange("p (k g c) -> p k g c", k=2, g=4)
        b_stg = self.b_ring
        S.op("dve", lambda e: e.tensor_copy(out=stg[:, :, :, 0:16], in_=self.cmphalo[:]), reads=[self.b_cmph], writes=b_stg)
        pi = 0
        for kind in range(4):
            for gg in range(4):
                col0 = [0, 256, 512, 1024][kind] + gg * 64
                for n in range(NNG):
                    pp, bp = self.pA[pi % 4], self.b_pA[pi % 4]
                    pi += 1
                    for kc in range(8):
                        S.op("pe", lambda e: e.matmul(pp[0:64, :], lhsT=self.wbig[:, kc * 1536 + col0: kc * 1536 + col0 + 64],
                                                      rhs=self.hT[:, kc, n * 512:(n + 1) * 512], start=(kc == 0), stop=(kc == 7)),
                             reads=[self.b_wbig] + self.b_hT[n * 4:(n + 1) * 4], writes=[bp], sig=(kc == 7))
                    if kind < 2:
                        dst, bd = stg[:, kind, gg, 16 + n * 512: 16 + (n + 1) * 512], b_stg
                    else:
                        dst, bd = kst[:, gg, n * 512:(n + 1) * 512], b_kst
                    if pi % 2 == 0:
                        S.op("act", lambda e: e.activation(out=dst, in_=pp[0:64, :], func=AF.Copy), reads=[bp], writes=bd)
                    else:
                        S.op("dve", lambda e: e.tensor_copy(out=dst, in_=pp[0:64, :]), reads=[bp], writes=bd)
            if kind >= 2:
                br = kind - 2
                S.dma("sp", self.kscr[br, g * NT:(g + 1) * NT, 0:64, :, :].rearrange("t p g k -> p g t k"),
                      kst.rearrange("p g (t k) -> p g t k", t=NT), "kst", reads=b_kst, writes=[self.b_kscr[br]])
        for br in range(2):
            for t in range(NT):
                for kc in range(8):
                    S.op("pe", lambda e: e.matmul(self.pM[:, 0:256], lhsT=self.hT[:, kc, t * 128:(t + 1) * 128],
                                                  rhs=self.wbig[:, kc * 1536 + (br + 1) * 512 + 256: kc * 1536 + (br + 1) * 512 + 512],
                                                  start=(kc == 0), stop=(kc == 7)),
                         reads=[self.b_wbig, self.b_hT[t]], writes=[self.b_pM], sig=(kc == 7))
                vi = self.vst_i % 2
                self.vst_i += 1
                dstv = self.vst[:, vi, :].rearrange("p (g c) -> p g c", c=65)[:, :, 0:64]
                S.op("act", lambda e: e.activation(out=dstv, in_=self.pM[:, 0:256].rearrange("p (g c) -> p g c", c=64), func=AF.Copy),
                     reads=[self.b_pM], writes=[self.b_vst[vi]])
                S.dma("sp", self.vscr[br, g * NT + t], self.vst[:, vi, :], f"vst{vi}", reads=[self.b_vst[vi]], writes=[self.b_vscr[br]])
        if g == 0:
            for kv in range(2):
                for hc in range(2):
                    for l in range(32):
                        c0 = (kv * 32 + l) * 256 + hc * 128
                        S.op("pe", lambda e: e.matmul(self.pM[:, 256 + kv * 2 + hc: 257 + kv * 2 + hc], lhsT=w1v[:, c0:c0 + 128],
                                                      rhs=self.posT[:, kv * 32 + l: kv * 32 + l + 1], start=(l == 0), stop=(l == 31)),
                             reads=b_w1 + [self.b_c2], writes=[self.b_pM], sig=(l == 31 and kv == 1 and hc == 1))
            S.op("dve", lambda e: e.tensor_copy(out=self.biash[:], in_=self.pM[:, 256:260]), reads=[self.b_pM], writes=[self.b_biash])
        n0 = max(0, 64 * g - 1)
        n1 = 64 * (g + 1) - 2
        nblk = n1 - n0 + 1
        colb = 16 * n0 - g * TG + 16
        ct = self.cmpt
        for kv in range(2):
            for gg in range(4):
                for hc in range(2):
                    pp, bp = self.pA[pi % 4], self.b_pA[pi % 4]
                    pi += 1
                    for l in range(32):
                        c0 = (kv * 32 + l) * 256 + hc * 128
                        S.op("pe", lambda e: e.matmul(pp[:, 0:nblk], lhsT=w1v[:, c0:c0 + 128],
                                                      rhs=stg[:, kv, gg, colb + l: colb + l + 16 * (nblk - 1) + 1: 16],
                                                      start=(l == 0), stop=(l == 31)),
                             reads=b_w1 + b_stg, writes=[bp], sig=(l == 31))
                    xs, x2, u, th, xh = ct[:, 0, 0:nblk], ct[:, 1, 0:nblk], ct[:, 2, 0:nblk], ct[:, 3, 0:nblk], ct[:, 4, 0:nblk]
                    bcol = self.biash[:, kv * 2 + hc: kv * 2 + hc + 1]
                    bc_ = [self.b_cmpt]
                    S.op("dve", lambda e: e.tensor_scalar(out=xs, in0=pp[:, 0:nblk], scalar1=bcol, scalar2=None, op0=ALU.add),
                         reads=[bp, self.b_biash], writes=bc_)
                    S.op("dve", lambda e: e.tensor_tensor(out=x2, in0=xs, in1=xs, op=ALU.mult), reads=bc_, writes=bc_)
                    S.op("dve", lambda e: e.tensor_scalar(out=u, in0=x2, scalar1=0.044715, scalar2=1.0, op0=ALU.mult, op1=ALU.add),
                         reads=bc_, writes=bc_)
                    S.op("dve", lambda e: e.tensor_tensor(out=u, in0=u, in1=xs, op=ALU.mult), reads=bc_, writes=bc_)
                    S.op("act", lambda e: e.activation(out=th, in_=u, func=AF.Tanh, scale=0.7978845608028654), reads=bc_, writes=bc_)
                    S.op("dve", lambda e: e.tensor_scalar(out=xh, in0=xs, scalar1=0.5, scalar2=None, op0=ALU.mult), reads=bc_, writes=bc_)
                    if kv == 0:
                        hdst, bh = self.hidK[:, hc, 0:nblk], [self.b_hidK]
                    else:
                        hdst, bh = self.hidV[:, hc, gg, n0:n0 + nblk], [self.b_hidV]
                    S.op("dve", lambda e: e.scalar_tensor_tensor(out=hdst, in0=th, scalar=1.0, in1=xh, op0=ALU.add, op1=ALU.mult),
                         reads=bc_, writes=bh)
                if kv == 0:
                    for hc in range(2):
                        S.op("pe", lambda e: e.matmul(self.pM[0:64, 0:nblk], lhsT=self.w2[:, hc * 64:(hc + 1) * 64],
                                                      rhs=self.hidK[:, hc, 0:nblk], start=(hc == 0), stop=(hc == 1)),
                             reads=[self.b_c2, self.b_hidK], writes=[self.b_pM], sig=(hc == 1))
                    S.op("act", lambda e: e.activation(out=self.kcT[0:64, gg, n0:n0 + nblk], in_=self.pM[0:64, 0:nblk], func=AF.Copy),
                         reads=[self.b_pM], writes=[self.b_kcT])
                else:
                    for nt in range(n0 // 128, n1 // 128 + 1):
                        for hc in range(2):
                            S.op("pe", lambda e: e.matmul(self.pM[:, 0:64], lhsT=self.hidV[:, hc, gg, nt * 128:(nt + 1) * 128],
                                                          rhs=self.w2[:, 128 + hc * 64: 128 + (hc + 1) * 64], start=(hc == 0), stop=(hc == 1)),
                                 reads=[self.b_c2, self.b_hidV], writes=[self.b_pM], sig=(hc == 1))
                        S.op("act", lambda e: e.activation(out=self.vc[:, nt, gg, 0:64], in_=self.pM[:, 0:64], func=AF.Copy),
                             reads=[self.b_pM], writes=[self.b_vc])
        S.op("dve", lambda e: e.tensor_copy(out=self.cmphalo[:], in_=stg[:, :, :, TG:TG + 16]), reads=b_stg, writes=[self.b_cmph])

    def kv_tile_load(self, br, kt, kview, vview):
        S = self.S
        i = self.kv_i % 3
        self.kv_i += 1
        kdst = kview[:, i]
        vdst = vview[:, i]
        bk, bv = self.b_kt[i], self.b_vt[i]
        S.dma("sp", kdst, self.kscr[br, kt], f"kt{i}", reads=[self.b_kscr[br], self.b_c2], writes=[bk])
        S.dma("sp", vdst, self.vscr[br, kt], f"vt{i}", reads=[self.b_vscr[br]], writes=[bv])
        return kdst, vdst, bk, bv

    def nsa(self, l, g):
        S, A = self.S, self.A
        lb = l - 2
        si = l * 3 + 1
        cbv, bcb = self.load_cb(si)
        S.dma("pool", self.wbig[:, 0:8576], A["bwin"][lb], "wbig", writes=[self.b_wbig])
        S.dma("pool", self.wbig[:, 8576:8576 + 8192], A["bwout"][lb], "wbig", writes=[self.b_wbig])
        allact = lambda j0, j1: [self.b_actT[j][n] for j in range(j0, j1) for n in range(NNG)]
        aflat = self.actT[:, :, :].rearrange("p a b -> p (a b)")
        QT = aflat[:, 0:16 * TG].rearrange("p (h t) -> p h t", h=16)
        b_QT = allact(0, 16)
        fm = aflat[0:64, 16 * TG:16 * TG + SEQ]
        b_fm = allact(16, 20)
        rflat = self.ring[:, :, :].rearrange("p a b -> p (a b)")
        kview = rflat[0:71, 0:1536].rearrange("p (s g k) -> p s g k", s=3, g=4)
        vview = rflat[:, 1536:1536 + 864].rearrange("p (s c) -> p s c", s=3)[:, :, 0:260]
        pview = rflat[:, 2432:2432 + 1536].rearrange("p (s c) -> p s c", s=3)
        MBT = rflat[0:64, 4032:4032 + 512].rearrange("p (g q) -> p g q", g=4)
        m01v = rflat[:, 4544:4544 + 1536].rearrange("p (s c) -> p s c", s=3)
        if not hasattr(self, "b_m01"):
            self.b_m01 = [Buf("m01_0"), Buf("m01_1"), Buf("m01_2")]
            self.m01_i = 0
        if not hasattr(self, "b_kt"):
            self.b_kt = [Buf(f"kt{i}") for i in range(3)]
            self.b_vt = [Buf(f"vt{i}") for i in range(3)]
            self.b_pt = [Buf(f"pt{i}") for i in range(3)]
            self.b_mbt = Buf("mbt")
            self.pt_i = 0
        for b in (self.b_kt + self.b_vt + self.b_pt + [self.b_mbt]):
            pass
        ring_guard = self.b_ring
        for eng in ("sp", "act", "dve", "pool"):
            S.wait_all(eng, self.b_ring)
        for t in range(NT):
            self.norm_tile(t, si)
        S.dma("pool", fm, A["fmat"], "fmat", writes=b_fm)
        S.dma("pool", QT[64:71, :, :], A["qaug"][:, :, g * TG:(g + 1) * TG], "qaug", writes=b_QT)
        pi = 0
        for hh in range(16):
            for n in range(NNG):
                pp, bp = self.pA[pi % 4], self.b_pA[pi % 4]
                pi += 1
                for kc in range(8):
                    S.op("pe", lambda e: e.matmul(pp[0:64, :], lhsT=self.wbig[:, kc * 1072 + hh * 64: kc * 1072 + hh * 64 + 64],
                                                  rhs=self.hT[:, kc, n * 512:(n + 1) * 512], start=(kc == 0), stop=(kc == 7)),
                         reads=[self.b_wbig] + self.b_hT[n * 4:(n + 1) * 4], writes=[bp], sig=(kc == 7))
                dst = QT[0:64, hh, n * 512:(n + 1) * 512]
                if pi % 2 == 0:
                    S.op("act", lambda e: e.activation(out=dst, in_=pp[0:64, :], func=AF.Copy, scale=0.125), reads=[bp], writes=b_QT)
                else:
                    S.op("dve", lambda e: e.tensor_scalar(out=dst, in0=pp[0:64, :], scalar1=0.125, scalar2=None, op0=ALU.mult),
                         reads=[bp], writes=b_QT)
        for t in range(NT):
            for kc in range(8):
                S.op("pe", lambda e: e.matmul(self.pM[:, 0:48], lhsT=self.hT[:, kc, t * 128:(t + 1) * 128],
                                              rhs=self.wbig[:, kc * 1072 + 1024: kc * 1072 + 1072], start=(kc == 0), stop=(kc == 7)),
                     reads=[self.b_wbig, self.b_hT[t]], writes=[self.b_pM], sig=(kc == 7))
            S.op("act", lambda e: e.activation(out=self.gates[:, t, :], in_=self.pM[:, 0:48], func=AF.Tanh, scale=0.5),
                 reads=[self.b_pM], writes=[self.b_gates])
            S.op("dve", lambda e: e.tensor_scalar(out=self.gates[:, t, :], in0=self.gates[:, t, :], scalar1=0.5, scalar2=0.5,
                                                  op0=ALU.mult, op1=ALU.add), reads=[self.b_gates], writes=[self.b_gates])
        o = self.tmp32
        bo = [self.b_tmp32]
        gs = self.gsm
        for t in range(NT):
            T = g * NT + t
            q0 = T * 128
            qs = slice(t * 128, (t + 1) * 128)
            mi = self.msk_i % 2
            self.msk_i += 1
            nts = [0] if T < 16 else [0, 1]
            for nt in nts:
                S.dma("pool", self.msk[:, mi, nt, :], A["tcmp"][:, q0 - 2048 * nt: q0 - 2048 * nt + 128], f"msk{mi}", writes=[self.b_msk[mi]])
            S.dma("pool", self.fbs[:, mi, :], A["fbias"][T], f"msk{mi}", writes=[self.b_msk[mi]])

            def bc4(ap2d):
                return ap2d.rearrange("p (o q) -> p o q", o=1).to_broadcast([ap2d.shape[0], 4, 128])

            def post_cmp(gg):
                for hb in range(2):
                    av = self.pY[:, hb * 512: hb * 512 + 258].rearrange("p (r c) -> p r c", c=129)
                    rd = gs[:, 0:2].rearrange("p (r o) -> p r o", o=1)
                    gr = gs[:, 2:4].rearrange("p (r o) -> p r o", o=1)
                    gcol = self.gates[:, t, gg * 4 + hb * 2: gg * 4 + hb * 2 + 2].rearrange("p (r o) -> p r o", o=1)
                    S.op("dve", lambda e: e.tensor_scalar(out=rd, in0=av[:, :, 64:65], scalar1=1e-30, scalar2=None, op0=ALU.max),
                         reads=[self.b_pY], writes=[self.b_gsm])
                    S.op("dve", lambda e: e.reciprocal(out=rd, in_=rd), reads=[self.b_gsm], writes=[self.b_gsm])
                    S.op("dve", lambda e: e.tensor_tensor(out=gr, in0=rd, in1=gcol, op=ALU.mult), reads=[self.b_gsm, self.b_gates], writes=[self.b_gsm])
                    oc = o[:, (gg * 4 + hb * 2) * 64:(gg * 4 + hb * 2 + 2) * 64].rearrange("p (r c) -> p r c", c=64)
                    S.op("dve", lambda e: e.tensor_tensor(out=oc, in0=av[:, :, 0:64], in1=gr.to_broadcast([128, 2, 64]), op=ALU.mult),
                         reads=[self.b_pY, self.b_gsm], writes=bo)
                    iw = self.impw[:, hb * 2:hb * 2 + 2, :]
                    S.op("dve", lambda e: e.tensor_tensor(out=iw, in0=av[:, :, 65:129], in1=rd.to_broadcast([128, 2, 64]), op=ALU.mult),
                         reads=[self.b_pY, self.b_gsm], writes=[self.b_impw])
                ig = self.imp[:, gg, :]
                S.op("dve", lambda e: e.tensor_tensor(out=ig, in0=self.impw[:, 0, :], in1=self.impw[:, 1, :], op=ALU.add),
                     reads=[self.b_impw], writes=[self.b_imp])
                S.op("dve", lambda e: e.tensor_tensor(out=ig, in0=ig, in1=self.impw[:, 2, :], op=ALU.add), reads=[self.b_impw, self.b_imp], writes=[self.b_imp])
                S.op("dve", lambda e: e.tensor_tensor(out=ig, in0=ig, in1=self.impw[:, 3, :], op=ALU.add), reads=[self.b_impw, self.b_imp], writes=[self.b_imp])
                S.op("dve", lambda e: e.tensor_tensor(out=ig, in0=ig, in1=self.fbs[:, mi, :], op=ALU.add), reads=[self.b_imp, self.b_msk[mi]], writes=[self.b_imp])
                S.op("dve", lambda e: e.max(out=self.mx[:, 0:8], in_=ig), reads=[self.b_imp], writes=[self.b_mx])
                wk = self.impw[:, 0, :]
                S.op("dve", lambda e: e.match_replace(out=wk, in_to_replace=self.mx[:, 0:8], in_values=ig, imm_value=-3.0e38),
                     reads=[self.b_imp, self.b_mx], writes=[self.b_impw])
                S.op("dve", lambda e: e.max(out=self.mx[:, 8:16], in_=wk), reads=[self.b_impw], writes=[self.b_mx])
                S.op("dve", lambda e: e.tensor_scalar(out=self.mbq[:, gg, :], in0=ig, scalar1=self.mx[:, 15:16], scalar2=None,
                                                      op0=ALU.is_ge), reads=[self.b_imp, self.b_mx], writes=[self.b_mbq])
                S.op("pe", lambda e: e.transpose(out=self.pT[0:64, gg * 128:(gg + 1) * 128], in_=self.mbq[:, gg, :], identity=self.ident[:]),
                     reads=[self.b_mbq, self.b_ident], writes=[self.b_pT], sig=True)
                S.op("act", lambda e: e.activation(out=MBT[:, gg, :], in_=self.pT[0:64, gg * 128:(gg + 1) * 128], func=AF.Copy),
                     reads=[self.b_pT], writes=[self.b_mbt] + ring_guard)

            acc_banks = [(self.pY[:, 0:260], self.b_pY), (self.pY[:, 512:772], self.b_pY), (self.pA[2][:, 0:260], self.b_pA[2]),
                         (self.pA[3][:, 0:260], self.b_pA[3])]

            def post_br(br, gg):
                accb, bacc = acc_banks[gg]
                av = accb.rearrange("p (r c) -> p r c", c=65)
                rd = gs[:, 8:12].rearrange("p (r o) -> p r o", o=1)
                gcol = self.gates[:, t, (br + 1) * 16 + gg * 4:(br + 1) * 16 + gg * 4 + 4].rearrange("p (r o) -> p r o", o=1)
                S.op("dve", lambda e: e.tensor_scalar(out=rd, in0=av[:, :, 64:65], scalar1=1e-30, scalar2=None, op0=ALU.max),
                     reads=[bacc], writes=[self.b_gsm])
                S.op("dve", lambda e: e.reciprocal(out=rd, in_=rd), reads=[self.b_gsm], writes=[self.b_gsm])
                S.op("dve", lambda e: e.tensor_tensor(out=rd, in0=rd, in1=gcol, op=ALU.mult), reads=[self.b_gsm, self.b_gates], writes=[self.b_gsm])
                ow = self.impw[:, :, :]
                S.op("dve", lambda e: e.tensor_tensor(out=ow, in0=av[:, :, 0:64], in1=rd.to_broadcast([128, 4, 64]), op=ALU.mult),
                     reads=[bacc, self.b_gsm], writes=[self.b_impw])
                oc = o[:, gg * 256:(gg + 1) * 256].rearrange("p (r c) -> p r c", c=64)
                S.op("pool", lambda e: e.tensor_tensor(out=oc, in0=oc, in1=ow, op=ALU.add), reads=[self.b_impw] + bo, writes=bo)

            steps = []
            for gg in range(4):
                accs = [self.pY[:, (r // 2) * 512 + (r % 2) * 129: (r // 2) * 512 + (r % 2) * 129 + 129] for r in range(4)]
                for ii, nt in enumerate(nts):
                    steps.append(dict(gg=gg, tile=None, selkt=None,
                                      kv=(lambda gg=gg, nt=nt: (self.kcT[0:71, gg, nt * 128:(nt + 1) * 128], [self.b_kcT, self.b_c2],
                                                               self.vc[:, nt, gg, :], [self.b_vc, self.b_c2])),
                                      mask=(lambda gg=gg, nt=nt: (self.ident[:], bc4(self.msk[:, mi, nt, :]), [self.b_ident, self.b_msk[mi]])),
                                      accs=accs, bacc=self.b_pY, first=(ii == 0), last=(ii == len(nts) - 1), start_r=(0, 2),
                                      post=((lambda gg=gg: post_cmp(gg)) if ii == len(nts) - 1 else None)))
            tiles = []
            for br in range(2):
                kts = list(range(0, T + 1)) if br == 0 else list(range(max(0, T - 4), T + 1))
                for ii, kt in enumerate(kts):
                    tidx = len(tiles)
                    tiles.append((br, kt))
                    for gg in range(4):
                        accb, bacc = acc_banks[gg]
                        accs = [accb[:, r * 65:(r + 1) * 65] for r in range(4)]
                        if kt == T:
                            mk = (lambda gg=gg: (self.ident[:], bc4(self.mbd[:, 0, :]), [self.b_ident, self.b_c2]))
                        elif br == 0:
                            mk = None
                        elif kt == T - 4:
                            mk = (lambda gg=gg: (self.ident[:], bc4(self.mbd[:, 1, :]), [self.b_ident, self.b_c2]))
                        else:
                            mk = None
                        last = (ii == len(kts) - 1)
                        steps.append(dict(gg=gg, tile=tidx, kv=None, mask=mk, accs=accs, bacc=bacc, first=(ii == 0), last=last,
                                          start_r=(0,), post=((lambda br=br, gg=gg: post_br(br, gg)) if last else None),
                                          selkt=(kt if (br == 0 and kt != T) else None)))
            loaded = {}
            mexp = {}
            pT32 = self.pT[:, :].bitcast(F32)
            sbanks = [(self.pA[0], self.b_pA[0]), (self.pA[1], self.b_pA[1]), (self.pM, self.b_pM)]
            if not hasattr(self, "sp_i"):
                self.sp_i = 0

            def ensure_load(idx):
                if idx < len(tiles) and idx not in loaded:
                    loaded[idx] = self.kv_tile_load(tiles[idx][0], tiles[idx][1], kview, vview)

            def stage_a(k, st_):
                gg = st_["gg"]
                if st_["tile"] is not None:
                    ensure_load(st_["tile"])
                    ensure_load(st_["tile"] + 1)
                    kdst, vdst, bk, bv = loaded[st_["tile"]]
                    kT_ap, bkl, v_ap, bvl = kdst[:, gg, :], [bk] + ring_guard, vdst[:, gg * 65:(gg + 1) * 65], [bv]
                else:
                    kT_ap, bkl, v_ap, bvl = st_["kv"]()
                st_["v"] = (v_ap, bvl)
                if st_["selkt"] is not None:
                    if gg == 0:
                        for kt_ in (st_["selkt"], st_["selkt"] + 1):
                            if kt_ >= T or kt_ in mexp:
                                continue
                            S.op("pe", lambda e: e.matmul(pT32, lhsT=fm[:, kt_ * 128:(kt_ + 1) * 128], rhs=MBT[:, :, :], start=True, stop=True),
                                 reads=b_fm + [self.b_mbt], writes=[self.b_pT], sig=True)
                            mi_ = self.m01_i % 3
                            self.m01_i += 1
                            S.op("dve", lambda e: e.tensor_copy(out=m01v[:, mi_, :], in_=pT32),
                                 reads=[self.b_pT], writes=[self.b_m01[mi_]] + ring_guard)
                            mexp[kt_] = mi_
                    st_["m01"] = mexp[st_["selkt"]]
                sp, bsp = sbanks[self.sp_i % 3]
                self.sp_i += 1
                st_["sp"] = (sp, bsp)
                mm = st_["mask"]() if st_["mask"] is not None else None
                rhs_q = QT[0:71, gg * 4:(gg + 1) * 4, qs]
                S.op("pe", lambda e: e.matmul(sp[:], lhsT=kT_ap, rhs=rhs_q, start=True, stop=(mm is None)),
                     reads=bkl + b_QT, writes=[bsp], sig=(mm is None))
                if mm is not None:
                    mlhs, mrhs, mb_ = mm
                    S.op("pe", lambda e: e.matmul(sp[:], lhsT=mlhs, rhs=mrhs, start=False, stop=True), reads=mb_, writes=[bsp], sig=True)

            def stage_bc(st_):
                sp, bsp = st_["sp"]
                v_ap, bvl = st_["v"]
                pi_ = self.pt_i % 3
                self.pt_i += 1
                pt = pview[:, pi_]
                S.op("act", lambda e: e.activation(out=pt, in_=sp[:], func=AF.Exp), reads=[bsp], writes=[self.b_pt[pi_]])
                if st_["selkt"] is not None:
                    mi_ = st_["m01"]
                    gg_ = st_["gg"]
                    pt4 = pt.rearrange("p (r q) -> p r q", r=4)
                    S.op("dve", lambda e: e.tensor_tensor(out=pt4, in0=pt4, in1=bc4(m01v[:, mi_, gg_ * 128:(gg_ + 1) * 128]), op=ALU.mult),
                         reads=[self.b_pt[pi_], self.b_m01[mi_]], writes=[self.b_pt[pi_]])
                for r in range(4):
                    S.op("pe", lambda e: e.matmul(st_["accs"][r], lhsT=pt[:, r * 128:(r + 1) * 128], rhs=v_ap,
                                                  start=(st_["first"] and r in st_["start_r"]), stop=st_["last"]),
                         reads=[self.b_pt[pi_]] + bvl, writes=[st_["bacc"]], sig=(st_["last"] and r == 3))
                if st_["post"] is not None:
                    st_["post"]()

            ncmp = 4 * len(nts)
            for lst in (steps[:ncmp], steps[ncmp:]):
                for k0 in range(min(2, len(lst))):
                    stage_a(k0, lst[k0])
                for k in range(len(lst)):
                    if k + 2 < len(lst):
                        stage_a(k + 2, lst[k + 2])
                    stage_bc(lst[k])
            if getattr(self, "dbg_o", False):
                if "dbg" not in S.semh:
                    S.newsem("dbg")
                S.dma("sp", self.out[g * TG + t * 128: g * TG + (t + 1) * 128, :], o[:], "dbg", reads=bo, writes=[self.b_out])
            xi = self.xn_i % 2
            self.xn_i += 1
            xn = self.xn[:, xi, :]
            S.op("act", lambda e: e.activation(out=xn, in_=o[:], func=AF.Copy), reads=bo, writes=[self.b_xn[xi]])
            for c in range(8):
                S.op("pe", lambda e: e.transpose(out=self.pT[:, c * 128:(c + 1) * 128], in_=xn[:, c * 128:(c + 1) * 128], identity=self.ident[:]),
                     reads=[self.b_xn[xi], self.b_ident], writes=[self.b_pT], sig=(c == 7))
            S.op("dve", lambda e: e.tensor_copy(out=self.hT[:, :, qs], in_=self.pT[:, :].rearrange("p (c q) -> p c q", c=8)),
                 reads=[self.b_pT], writes=[self.b_hT[t]])
        for t in range(NT):
            yb = self.ybank(t)
            for dh in range(2):
                for kc in range(8):
                    S.op("pe", lambda e: e.matmul(yb[dh], lhsT=self.hT[:, kc, t * 128:(t + 1) * 128],
                                                  rhs=self.wbig[:, 8576 + kc * D + dh * 512: 8576 + kc * D + (dh + 1) * 512],
                                                  start=(kc == 0), stop=(kc == 7)),
                         reads=[self.b_hT[t], self.b_wbig], writes=yb[2], sig=(kc == 7))
            self.post_tile(t, cbv, bcb)
        S.wait_all("pool", self.b_kt + self.b_vt + self.b_pt + [self.b_mbt] + self.b_m01)

_PROG_CACHE = {}


def _device_inputs(inp, ntok=SEQ):
    shared = host_layout(inp)
    shared.update(host_constants())
    maps = []
    for b in range(8):
        f = dict(shared)
        f.update(host_percore(inp, b, ntok))
        maps.append(f)
    return maps


def kernel(**inputs):
    inp = {k: np.asarray(v) for k, v in inputs.items()}
    in_maps = _device_inputs(inp)
    shapes = {k: v.shape for k, v in in_maps[0].items()}
    prog = Prog()
    nc = prog.build(shapes)
    res = run_bass_kernel_spmd(nc, in_maps, core_ids=list(range(8)))
    out = np.stack([np.asarray(r["out"], dtype=np.float32) for r in res.results], axis=0)
    return out
```
